# Optimizing a Trainium2 kernel written in Bass

```python
import jax, jax.numpy as jnp
from jax import lax
import numpy as np

D_MODEL = 2048
BATCH = 4
SEQ = 2048
DEPTH = 4

CTX_LEN = 256
GRID_W = 64
Q_BLOCK = 128
ROPE_THETA = 10000.0
EPS = 1e-6

GQA_HEADS = 8
GQA_KV_HEADS = 2
GQA_GROUP = GQA_HEADS // GQA_KV_HEADS
GQA_HEAD_DIM = 128
GQA_WIDTH = GQA_HEADS * GQA_HEAD_DIM
MLA_HEADS = 8
MLA_NOPE_DIM = 128
MLA_ROPE_DIM = 64
MLA_QK_DIM = MLA_NOPE_DIM + MLA_ROPE_DIM
MLA_V_DIM = 128
MLA_Q_RANK = 512
MLA_KV_RANK = 512
MLA_WIDTH = MLA_HEADS * MLA_V_DIM
D_MIX = GQA_WIDTH + MLA_WIDTH
IN_SIZES = (GQA_WIDTH, GQA_KV_HEADS * GQA_HEAD_DIM, GQA_KV_HEADS * GQA_HEAD_DIM, GQA_WIDTH,
            MLA_Q_RANK, MLA_KV_RANK, MLA_ROPE_DIM, MLA_WIDTH)
D_IN = GQA_WIDTH + 2 * GQA_KV_HEADS * GQA_HEAD_DIM + GQA_WIDTH + MLA_Q_RANK + MLA_KV_RANK + MLA_ROPE_DIM + MLA_WIDTH

kernel_name = "hybrid_gqa_mla_prefix_dit"


def rms_norm(x, g):
    xf = x.astype(jnp.float32)
    y = xf * lax.rsqrt(jnp.mean(xf * xf, axis=-1, keepdims=True) + EPS)
    return (y * g.astype(jnp.float32)).astype(x.dtype)


def axial_rope_tables(n_tokens, rot_dim, dtype):
    rows = n_tokens // GRID_W
    row = jnp.repeat(jnp.arange(rows, dtype=jnp.float32), GRID_W)
    col = jnp.tile(jnp.arange(GRID_W, dtype=jnp.float32), rows)
    n_freq = rot_dim // 4
    inv = ROPE_THETA ** (-jnp.arange(n_freq, dtype=jnp.float32) / n_freq)
    ang = jnp.concatenate([row[:, None] * inv[None], col[:, None] * inv[None]], axis=-1)
    return jnp.cos(ang).astype(dtype), jnp.sin(ang).astype(dtype)


def apply_rope(x, tables):
    cos, sin = tables
    cos, sin = cos[None, :, None, :], sin[None, :, None, :]
    x1, x2 = jnp.split(x, 2, axis=-1)
    return jnp.concatenate([x1 * cos - x2 * sin, x2 * cos + x1 * sin], axis=-1)


def block_attention(q, k, v):
    B, Sq, Hk, G, Dk = q.shape
    nb = Sq // Q_BLOCK
    scale = Dk ** -0.5
    qb = q.reshape(B, nb, Q_BLOCK, Hk, G, Dk).swapaxes(0, 1)

    def one_block(qblk):
        s = jnp.einsum('bqhgd,bshd->bhgqs', qblk, k).astype(jnp.float32) * scale
        p = jax.nn.softmax(s, axis=-1).astype(v.dtype)
        return jnp.einsum('bhgqs,bshd->bqhgd', p, v)

    o = lax.map(one_block, qb)
    return o.swapaxes(0, 1).reshape(B, Sq, Hk * G * v.shape[-1])


def mixer_inputs(h, w_in, q_gain, k_gain, cq_gain, ckv_gain, w_uq, w_ukv, rope_gqa, rope_mla):
    B, S, _ = h.shape
    p = h @ w_in
    offsets = [int(o) for o in np.cumsum(IN_SIZES)[:-1]]
    q, k, v, g_gqa, cq, ckv, kr, g_mla = jnp.split(p, offsets, axis=-1)
    q = rms_norm(q.reshape(B, S, GQA_HEADS, GQA_HEAD_DIM), q_gain)
    k = rms_norm(k.reshape(B, S, GQA_KV_HEADS, GQA_HEAD_DIM), k_gain)
    v = v.reshape(B, S, GQA_KV_HEADS, GQA_HEAD_DIM)
    qm = (rms_norm(cq, cq_gain) @ w_uq).reshape(B, S, MLA_HEADS, MLA_QK_DIM)
    q_nope, q_rope = qm[..., :MLA_NOPE_DIM], qm[..., MLA_NOPE_DIM:]
    kvm = (rms_norm(ckv, ckv_gain) @ w_ukv).reshape(B, S, MLA_HEADS, MLA_NOPE_DIM + MLA_V_DIM)
    k_nope, v_mla = kvm[..., :MLA_NOPE_DIM], kvm[..., MLA_NOPE_DIM:]
    kr = kr.reshape(B, S, 1, MLA_ROPE_DIM)
    if rope_gqa is not None:
        q = apply_rope(q, rope_gqa)
        k = apply_rope(k, rope_gqa)
        q_rope = apply_rope(q_rope, rope_mla)
        kr = apply_rope(kr, rope_mla)
    q = q.reshape(B, S, GQA_KV_HEADS, GQA_GROUP, GQA_HEAD_DIM)
    q_mla = jnp.concatenate([q_nope, q_rope], axis=-1)[:, :, :, None, :]
    k_mla = jnp.concatenate([k_nope, jnp.broadcast_to(kr, (B, S, MLA_HEADS, MLA_ROPE_DIM))], axis=-1)
    return (q, k, v, g_gqa), (q_mla, k_mla, v_mla, g_mla)


def merge_groups(o_gqa, g_gqa, o_mla, g_mla, w_out):
    y = jnp.concatenate([o_gqa * jax.nn.silu(g_gqa), o_mla * jax.nn.silu(g_mla)], axis=-1)
    return y @ w_out


def setup_inputs(seed: int = 0) -> dict:
    key = jax.random.key(seed)
    ks = jax.random.split(key, 16)
    f = jnp.float32
    nrm = lambda k, shape, s: jax.random.normal(k, shape, f) * s
    return {
        "x": nrm(ks[0], (BATCH, SEQ, D_MODEL), 1.0),
        "c": nrm(ks[1], (BATCH, D_MODEL), 1.0),
        "ctx": nrm(ks[2], (BATCH, CTX_LEN, D_MODEL), 1.0),
        "c_ctx": nrm(ks[3], (D_MODEL,), 1.0),
        "w_ada": nrm(ks[4], (DEPTH, D_MODEL, 3 * D_MODEL), 0.5 * D_MODEL ** -0.5),
        "b_ada": nrm(ks[5], (DEPTH, 3 * D_MODEL), 0.01),
        "norm_g": 1.0 + nrm(ks[6], (DEPTH, D_MODEL), 0.02),
        "w_in": nrm(ks[7], (DEPTH, D_MODEL, D_IN), D_MODEL ** -0.5),
        "q_gain": 1.0 + nrm(ks[8], (DEPTH, GQA_HEAD_DIM), 0.02),
        "k_gain": 1.0 + nrm(ks[9], (DEPTH, GQA_HEAD_DIM), 0.02),
        "cq_gain": 1.0 + nrm(ks[10], (DEPTH, MLA_Q_RANK), 0.02),
        "ckv_gain": 1.0 + nrm(ks[11], (DEPTH, MLA_KV_RANK), 0.02),
        "w_uq": nrm(ks[12], (DEPTH, MLA_Q_RANK, MLA_HEADS * MLA_QK_DIM), MLA_Q_RANK ** -0.5),
        "w_ukv": nrm(ks[13], (DEPTH, MLA_KV_RANK, MLA_HEADS * (MLA_NOPE_DIM + MLA_V_DIM)), MLA_KV_RANK ** -0.5),
        "w_out": nrm(ks[14], (DEPTH, D_MIX, D_MODEL), D_MIX ** -0.5),
        "final_g": 1.0 + nrm(ks[15], (D_MODEL,), 0.02),
    }


def reference(x, c, ctx, c_ctx, w_ada, b_ada, norm_g, w_in, q_gain, k_gain, cq_gain, ckv_gain,
              w_uq, w_ukv, w_out, final_g):
    n_tok = x.shape[1]
    ROWS = n_tok // GRID_W
    rope_gqa = axial_rope_tables(ROWS * GRID_W, GQA_HEAD_DIM, x.dtype)
    rope_mla = axial_rope_tables(ROWS * GRID_W, MLA_ROPE_DIM, x.dtype)
    xc = ctx
    sc = jax.nn.silu(c)
    sc_ctx = jax.nn.silu(c_ctx)
    for l in range(DEPTH):
        shift, scale, gate = jnp.split(sc @ w_ada[l] + b_ada[l], 3, axis=-1)
        shift_c, scale_c, gate_c = jnp.split(sc_ctx @ w_ada[l] + b_ada[l], 3, axis=-1)
        h_lat = rms_norm(x, norm_g[l]) * (1.0 + scale[:, None]) + shift[:, None]
        h_ctx = rms_norm(xc, norm_g[l]) * (1.0 + scale_c) + shift_c
        lat_a, lat_b = mixer_inputs(h_lat, w_in[l], q_gain[l], k_gain[l], cq_gain[l], ckv_gain[l],
                                    w_uq[l], w_ukv[l], rope_gqa, rope_mla)
        ctx_a, ctx_b = mixer_inputs(h_ctx, w_in[l], q_gain[l], k_gain[l], cq_gain[l], ckv_gain[l],
                                    w_uq[l], w_ukv[l], None, None)
        o_a = block_attention(lat_a[0], jnp.concatenate([ctx_a[1], lat_a[1]], axis=1),
                              jnp.concatenate([ctx_a[2], lat_a[2]], axis=1))
        o_b = block_attention(lat_b[0], jnp.concatenate([ctx_b[1], lat_b[1]], axis=1),
                              jnp.concatenate([ctx_b[2], lat_b[2]], axis=1))
        x = x + gate[:, None] * merge_groups(o_a, lat_a[3], o_b, lat_b[3], w_out[l])
        if l < DEPTH - 1:
            oc_a = block_attention(ctx_a[0], ctx_a[1], ctx_a[2])
            oc_b = block_attention(ctx_b[0], ctx_b[1], ctx_b[2])
            xc = xc + gate_c * merge_groups(oc_a, ctx_a[3], oc_b, ctx_b[3], w_out[l])
    return rms_norm(x, final_g)
```

```python
import bisect
from contextlib import ExitStack

import numpy as np
import concourse.bass as bass
import concourse.mybir as mybir
from concourse.bass_utils import run_bass_kernel_spmd

F32 = mybir.dt.float32
BF16 = mybir.dt.bfloat16
ALU = mybir.AluOpType
AF = mybir.ActivationFunctionType

D = 2048
KC = 16
DEPTH = 4
TO = 1152
TA = 2304
SEQ = 2048
CTX = 256
GRID_W = 64
EPS = 1e-6
OWN_BLOCKS = [(0, 128), (128, 512), (640, 512)]
KEY_BLOCKS = [(r, o, n) for r in range(2) for (o, n) in OWN_BLOCKS]
C_QG = 0
C_CQ = 2048
C_GM = 2560
C_KV = 3584
C_CKV = 4096
C_KR = 4608
CC_ID = 0
CC_SW128 = 128
CC_SW64 = 256
CC_L = 320
CC_LW = 76
CC_FG = CC_L + DEPTH * CC_LW
CC_CB = CC_FG + 16
CC_CC = CC_CB + 16
NCC = CC_CC + 16
EPOCH = 8000


class Prog:
    def __init__(self):
        self.ops = []
        self.last_w = {}
        self.readers = {}
        self.chan_hist = {}
        self.arena_gen = {}
        self.arena_keys = {}
        self.arena_fence = {}

    def akey(self, arena, sub):
        g = self.arena_gen.setdefault(arena, 0)
        k = ("A", arena, g, sub)
        ks = self.arena_keys.setdefault((arena, g), set())
        if k not in ks:
            ks.add(k)
            f = self.arena_fence.get((arena, g))
            if f is not None and k not in self.last_w:
                self.last_w[k] = f
                self.readers[k] = []
        return k

    def add(self, eng, fn, r=(), w=(), chan=None, inc=16):
        oid = len(self.ops)
        deps = set()
        for k in r:
            lw = self.last_w.get(k)
            if lw is not None:
                deps.add(lw)
        for k in w:
            lw = self.last_w.get(k)
            if lw is not None:
                deps.add(lw)
            deps.update(self.readers.get(k, ()))
        deps.discard(oid)
        for k in r:
            self.readers.setdefault(k, []).append(oid)
        for k in w:
            self.last_w[k] = oid
            self.readers[k] = []
        self.ops.append(dict(eng=eng, fn=fn, deps=deps, chan=chan, inc=inc, signal=False))
        return oid

    def fence(self, arena, tile):
        g = self.arena_gen.setdefault(arena, 0)
        keys = list(self.arena_keys.get((arena, g), ()))
        oid = self.add("dve", lambda e: e.memset(tile, 0.0), r=(), w=keys + [("fence", arena)])
        self.arena_gen[arena] = g + 1
        self.arena_fence[(arena, g + 1)] = oid
        return oid

    def finalize(self, nc, es):
        ops = self.ops
        for o in ops:
            for d in o["deps"]:
                if ops[d]["eng"] == "pe" and o["eng"] == "pe" and ops[d]["chan"] is None:
                    continue
                ops[d]["signal"] = True
        cnt = {}
        self.sems = {}
        for i, o in enumerate(ops):
            if o["chan"] is not None:
                c = o["chan"]
                cum = (self.chan_hist[c][-1][1] if c in self.chan_hist else 0) + o["inc"]
                self.chan_hist.setdefault(c, []).append((i, cum))
                o["sig"] = (("chan", c), cum)
                if ("chan", c) not in self.sems:
                    self.sems[("chan", c)] = es.enter_context(nc.semaphore("c_" + c))
            elif o["signal"]:
                n = cnt.get(o["eng"], 0)
                ep, v = divmod(n, EPOCH)
                cnt[o["eng"]] = n + 1
                sk = ("eng", o["eng"], ep)
                if sk not in self.sems:
                    self.sems[sk] = es.enter_context(nc.semaphore("e_%s_%d" % (o["eng"], ep)))
                o["sig"] = (sk, v + 1)
        for i, o in enumerate(ops):
            waits = {}
            for d in o["deps"]:
                od = ops[d]
                if od["eng"] == "pe" and o["eng"] == "pe" and od["chan"] is None:
                    continue
                if od["chan"] is not None:
                    hist = self.chan_hist[od["chan"]]
                    j = bisect.bisect_left(hist, (i, -1)) - 1
                    sk, val = ("chan", od["chan"]), hist[j][1]
                else:
                    sk, val = od["sig"]
                if waits.get(sk, 0) < val:
                    waits[sk] = val
            o["waits"] = waits

    def emit(self, engname, e):
        seen = {}
        for o in self.ops:
            if o["eng"] != engname:
                continue
            for sk, val in o["waits"].items():
                if seen.get(sk, 0) >= val:
                    continue
                seen[sk] = val
                e.wait_ge(self.sems[sk], val)
            ins = o["fn"](e)
            if o["chan"] is not None:
                ins.then_inc(self.sems[o["sig"][0]], o["inc"])
            elif o["signal"]:
                ins.then_inc(self.sems[o["sig"][0]], 1)


def build_program(depth=DEPTH, dbg=None, wd=DEPTH, nph=4):
    nc = bass.Bass("TRN2", target_bir_lowering=False)
    x_in = nc.dram_tensor("x_own", [TO, D], F32, kind="ExternalInput").ap()
    cst_in = nc.dram_tensor("consts", [128, NCC], F32, kind="ExternalInput").ap()
    tabq = nc.dram_tensor("tabq", [128, 4, TO], F32, kind="ExternalInput").ap()
    tabk = nc.dram_tensor("tabk", [128, 4, TA], F32, kind="ExternalInput").ap()
    w_ada = nc.dram_tensor("w_ada", [wd, D, 3 * D], F32, kind="ExternalInput").ap()
    w_in = nc.dram_tensor("w_in_p", [wd, D, 4672], F32, kind="ExternalInput").ap()
    w_mla = nc.dram_tensor("w_mla_p", [wd, 8, 512, 448], F32, kind="ExternalInput").ap()
    w_out = nc.dram_tensor("w_out", [wd, D, D], F32, kind="ExternalInput").ap()
    out = nc.dram_tensor("out", [1024, D], F32, kind="ExternalOutput").ap()
    xT = nc.dram_tensor("xT_s", [D, TO], F32)
    hx_in = [nc.dram_tensor("hx_in%d" % i, [D, n], BF16) for i, (o, n) in enumerate(OWN_BLOCKS)]
    hx_all = [nc.dram_tensor("hx_all%d" % i, [2 * D, n], BF16) for i, (o, n) in enumerate(OWN_BLOCKS)]
    yT = nc.dram_tensor("yT_s", [D, TO], BF16)
    dbg_out = None
    if dbg is not None:
        dbg_out = nc.dram_tensor("dbg", [D, TO], F32, kind="ExternalOutput").ap()

    P = Prog()
    es = ExitStack()
    with es:
        def sb(name, shape, dt):
            return es.enter_context(nc.sbuf_tensor(name, shape, dt))

        arenaA = sb("arenaA", [128, KC * TO], BF16)
        arenaC = sb("arenaC", [128, 9216], BF16)
        ckvn = sb("ckvn", [128, 4, TA], BF16)
        krT = sb("krT", [128, TA], BF16)
        Qg = [sb("Qg%d" % i, [128, TO], BF16) for i in range(2)]
        Qn = [sb("Qn%d" % i, [128, TO], BF16) for i in range(2)]
        Qr = [sb("Qr%d" % i, [128, TO], BF16) for i in range(2)]
        sg = [sb("sg%d" % i, [128, TO], BF16) for i in range(2)]
        cqn = sb("cqn", [128, 4, TO], BF16)
        WA = [sb("WA%d" % i, [128, KC, 512], BF16) for i in range(2)]
        WB = [sb("WB%d" % i, [128, 4, 448], BF16) for i in range(2)]
        tabs = [sb("tab%d" % i, [128, 2, 512], F32) for i in range(2)]
        NF = 12
        Fp = sb("Fp", [128, NF, 512], F32)
        NH = 6
        Hp = sb("Hp", [128, NH, 512], BF16)
        hst = sb("hst", [128, 4, 512], BF16)
        rstd_b = sb("rstd_b", [128, TO], F32)
        accs = [sb("acc%d" % i, [128, 512], F32) for i in range(2)]
        accd = [sb("accd%d" % i, [128, 512], F32) for i in range(2)]
        cst = sb("cst", [128, NCC], F32)
        mod = sb("mod", [128, DEPTH, 48, 2], F32)
        gs = sb("gs", [128, DEPTH, 16, 2], F32)
        ones = sb("ones", [128, 128], BF16)
        onesf = sb("onesf", [128, 128], F32)
        scT = sb("scT", [128, 16, 2], BF16)
        mrow = sb("mrow", [2, 512], F32)
        scr = sb("scr", [128, 8], F32)
        ps = [es.enter_context(nc.psum_tensor("ps%d" % i, [128, 512], F32)) for i in range(8)]

        hb_v = [arenaA[:, i * 8192:(i + 1) * 8192].rearrange("p (k t) -> p k t", k=KC) for i in range(2)]
        hTo = arenaA[:, :].rearrange("p (k t) -> p k t", k=KC)
        KgT = arenaC[:, 0:4608].rearrange("p (c t) -> p c t", c=2)
        Vg = arenaC[:, 4608:9216].rearrange("p (t c) -> p t c", t=18)
        Kn = [arenaC[:, i * 2304:(i + 1) * 2304] for i in range(2)]
        Vm = [arenaC[:, 4608 + i * 2304:4608 + (i + 1) * 2304].rearrange("p (t c) -> p t c", t=18) for i in range(2)]

        ident = cst[:, CC_ID:CC_ID + 128]
        psw128 = cst[:, CC_SW128:CC_SW128 + 128]
        psw64 = cst[0:64, CC_SW64:CC_SW64 + 64]

        def lc(l, off, n=1):
            b = CC_L + l * CC_LW + off
            return cst[:, b:b + n]

        st = dict(f=0, h=0, g=0, tab=0, s=0, o=0, wa=0)

        def Ft():
            i = st["f"] % NF
            st["f"] += 1
            return Fp[:, i, :], ("F", i)

        def Ht():
            i = st["h"] % NH
            st["h"] += 1
            return Hp[:, i, :], ("H", i)

        gpool = [0, 1]

        def Gb():
            i = gpool[st["g"] % len(gpool)]
            st["g"] += 1
            return ps[i], ("ps", i)

        def set_gpool(lst):
            gpool[:] = lst
            st["g"] = 0

        def wa_slot():
            i = st["wa"] % 2
            st["wa"] += 1
            return WA[i], ("WA", i), "WA%d" % i

        def tab_slot():
            i = st["tab"] % 2
            st["tab"] += 1
            return tabs[i], ("tab", i), "tab%d" % i

        def mm(out_, lhsT, rhs, start, stop, r, w):
            P.add("pe", lambda e: e.matmul(out=out_, lhsT=lhsT, rhs=rhs, start=start, stop=stop), r, w)

        def tr(out_, in_, idn, r, w):
            P.add("pe", lambda e: e.transpose(out=out_, in_=in_, identity=idn), r, w)

        def act(out_, in_, func, r, w, scale=1.0, bias=0.0):
            P.add("act", lambda e: e.activation(out=out_, in_=in_, func=func, bias=bias, scale=scale), r, w)

        def tt(out_, in0, in1, op, r, w):
            P.add("dve", lambda e: e.tensor_tensor(out=out_, in0=in0, in1=in1, op=op), r, w)

        def ts(out_, in0, s1, s2, op0, op1, r, w):
            if op1 is None:
                P.add("dve", lambda e: e.tensor_scalar(out=out_, in0=in0, scalar1=s1, scalar2=0.0, op0=op0, op1=ALU.add), r, w)
            else:
                P.add("dve", lambda e: e.tensor_scalar(out=out_, in0=in0, scalar1=s1, scalar2=s2, op0=op0, op1=op1), r, w)

        def stt(out_, in0, scalar, in1, op0, op1, r, w):
            P.add("dve", lambda e: e.scalar_tensor_tensor(out=out_, in0=in0, scalar=scalar, in1=in1, op0=op0, op1=op1), r, w)

        def recip(out_, in_, r, w):
            P.add("dve", lambda e: e.reciprocal(out=out_, in_=in_), r, w)

        def vcopy(out_, in_, r, w):
            P.add("dve", lambda e: e.tensor_copy(out=out_, in_=in_), r, w)

        def dma(eng, out_, in_, r, w, chan):
            P.add(eng, lambda e: e.dma_start(out=out_, in_=in_), r, w, chan=chan)

        pend = []
        gc = [0]

        def defer(fn, lag=1):
            pend.append((gc[0] + lag, fn))

        def tick():
            gc[0] += 1
            run = [p for p in pend if p[0] <= gc[0]]
            pend[:] = [p for p in pend if p[0] > gc[0]]
            for _, fn in run:
                fn()

        def flush():
            while pend:
                tick()

        def ptt(out_, in0, in1, op, r, w):
            P.add("pool", lambda e: e.tensor_tensor(out=out_, in0=in0, in1=in1, op=op), r, w)

        def pcopy(out_, in_, r, w):
            P.add("pool", lambda e: e.tensor_copy(out=out_, in_=in_), r, w)

        def rstd_from(ss_ap, sskey, n, nfeat, out_ap, outkey):
            act(out_ap, ss_ap, AF.Sqrt, [sskey], [outkey], scale=1.0 / nfeat, bias=EPS)
            recip(out_ap, out_ap, [outkey], [outkey])

        CST = ("cst",)

        def normrope(psA, pskey, n, F, tab, tabkey, out_ap, outkeys, gain=None, gain_sw=None, norm=False):
            raw, rk = Ft()
            act(raw[0:F, 0:n], psA, AF.Copy, [pskey], [rk])
            sq, sk = (None, None)
            if norm:
                sq, sk = Ht()
                tt(sq[0:F, 0:n], raw[0:F, 0:n], raw[0:F, 0:n], ALU.mult, [rk], [sk])

            def tail():
                if norm:
                    ssb, ssk = Gb()
                    mm(ssb[0:F, 0:n], ones[0:F, 0:F], sq[0:F, 0:n], True, True, [sk, ("ones",)], [ssk])
                    rs, rsk = Ft()
                    rstd_from(ssb[0:F, 0:n], ssk, n, F, rs[0:F, 0:n], rsk)
                swb, swk = Gb()
                pw = psw128 if F == 128 else psw64
                mm(swb[0:F, 0:n], pw, raw[0:F, 0:n], True, True, [rk, CST], [swk])
                t1, t1k = Ft()
                t2, t2k = Ft()
                if gain is not None:
                    stt(t1[0:F, 0:n], raw[0:F, 0:n], gain, tab[0:F, 0, 0:n], ALU.mult, ALU.mult, [rk, tabkey, CST], [t1k])
                    stt(t2[0:F, 0:n], swb[0:F, 0:n], gain_sw, tab[0:F, 1, 0:n], ALU.mult, ALU.mult, [swk, tabkey, CST], [t2k])
                else:
                    tt(t1[0:F, 0:n], raw[0:F, 0:n], tab[0:F, 0, 0:n], ALU.mult, [rk, tabkey], [t1k])
                    tt(t2[0:F, 0:n], swb[0:F, 0:n], tab[0:F, 1, 0:n], ALU.mult, [swk, tabkey], [t2k])
                if norm:
                    tt(t1[0:F, 0:n], t1[0:F, 0:n], t2[0:F, 0:n], ALU.add, [t1k, t2k], [t1k])
                    tt(out_ap, t1[0:F, 0:n], rs[0:F, 0:n], ALU.mult, [t1k, rsk], outkeys)
                else:
                    tt(out_ap, t1[0:F, 0:n], t2[0:F, 0:n], ALU.add, [t1k, t2k], outkeys)
            defer(tail, 2)

        def norm4(src_fn, srckeys, n, wslot, wkey, gain_off, l, out_fn, outkeys):
            raws = []
            ssb, ssk = ps[6], ("ps", 6)
            for c in range(4):
                pb, pk = Gb()
                for kc in range(KC):
                    mm(pb[:, 0:n], wslot[:, kc, c * 128:(c + 1) * 128], src_fn(kc), kc == 0, kc == KC - 1,
                       [wkey] + srckeys, [pk])
                raw, rk = Ft()
                act(raw[:, 0:n], pb[:, 0:n], AF.Copy, [pk], [rk])
                sq, sk = Ht()
                tt(sq[:, 0:n], raw[:, 0:n], raw[:, 0:n], ALU.mult, [rk], [sk])
                defer(lambda sq=sq, sk=sk, c=c: mm(ssb[:, 0:n], ones[:, :], sq[:, 0:n], c == 0, c == 3, [sk, ("ones",)], [ssk]), 2)
                tick()
                raws.append((raw, rk))
            flush()
            rs, rsk = Ft()
            rstd_from(ssb[:, 0:n], ssk, n, 512, rs[:, 0:n], rsk)
            for c in range(4):
                raw, rk = raws[c]
                stt(out_fn(c), raw[:, 0:n], lc(l, gain_off + c), rs[:, 0:n], ALU.mult, ALU.mult, [rk, rsk, CST], outkeys)

        dma("sp", cst[:, :], cst_in[:, :], [], [CST], "cst")
        P.add("dve", lambda e: e.memset(ones[:, :], 1.0), [], [("ones",)])
        P.add("dve", lambda e: e.memset(onesf[:, :], 1.0), [], [("onesf",)])
        P.add("dve", lambda e: e.memset(krT[:, :], 0.0), [], [("krT",)])
        for i in range(2):
            P.add("dve", lambda e, i=i: e.memset(Qr[i][:, :], 0.0), [], [("Qr", i)])
        f0, f0k = Ft()
        act(f0[:, 0:32], cst[:, CC_CB:CC_CB + 32], AF.Exp, [CST], [f0k], scale=-1.0)
        ts(f0[:, 0:32], f0[:, 0:32], 1.0, None, ALU.add, None, [f0k], [f0k])
        recip(f0[:, 0:32], f0[:, 0:32], [f0k], [f0k])
        for j in range(2):
            tt(scT[:, :, j], f0[:, j * 16:(j + 1) * 16], cst[:, CC_CB + j * 16:CC_CB + (j + 1) * 16], ALU.mult,
               [f0k, CST], [("scT",)])

        set_gpool([0, 1, 2, 3, 4, 5])
        xT_v = xT.ap().rearrange("(k p) t -> p k t", p=128)
        for t in range(9):
            xs_keys = [("F", 4 + i) for i in range(4)]
            xo_keys = [("F", 8 + i) for i in range(4)]
            dma("sp", Fp[:, 4:8, :], x_in[t * 128:(t + 1) * 128, :].rearrange("t (a b) -> t a b", a=4), [], xs_keys, "xs")
            for kg in range(4):
                pb, pk = Gb()
                for j in range(4):
                    k = kg * 4 + j
                    tr(pb[:, j * 128:(j + 1) * 128], Fp[:, 4 + kg, j * 128:(j + 1) * 128], ident, xs_keys + [CST], [pk])
                act(Fp[:, 8 + kg, :], pb[:, :], AF.Copy, [pk], [xo_keys[kg]])
            dma("sp", xT_v[:, :, t * 128:(t + 1) * 128],
                Fp[:, 8:12, :].rearrange("p a (j t) -> p (a j) t", j=4), xo_keys, [("xT", k) for k in range(KC)], "xTst")

        def ada_group(l, cg):
            slot, wkey, wchan = wa_slot()
            dma("pool", slot[:, :, :], w_ada[l][:, cg * 512:(cg + 1) * 512].rearrange("(k p) c -> p k c", p=128),
                [], [wkey], wchan)
            pb, pk = Gb()
            for kc in range(KC):
                mm(pb[0:2, 0:512], scT[:, kc, :], slot[:, kc, :], kc == 0, kc == KC - 1, [wkey, ("scT",)], [pk])
            act(mrow[0:2, 0:512], pb[0:2, 0:512], AF.Copy, [pk], [("mrow",)])
            for j in range(4):
                ch = cg * 4 + j
                pb2, pk2 = Gb()
                tr(pb2[:, 0:2], mrow[0:2, j * 128:(j + 1) * 128], ident[0:2, 0:2], [("mrow",), CST], [pk2])
                ts(mod[:, l, ch, :], pb2[:, 0:2], lc(l, 16 + ch), None, ALU.add, None, [pk2, CST], [("mod", l)])

        def ada_gs(l):
            for k in range(KC):
                ts(gs[:, l, k, :], mod[:, l, 16 + k, :], 1.0, lc(l, k), ALU.add, ALU.mult, [("mod", l), CST], [("gs", l)])

        if depth >= 1:
            for cg in range(12):
                ada_group(0, cg)
            ada_gs(0)

        def rms_stats(l):
            accs = [(ps[5], ("ps", 5)), (ps[6], ("ps", 6)), (ps[7], ("ps", 7))]
            for k in range(KC):
                for bi, (o, n) in enumerate([(0, 384), (384, 384), (768, 384)]):
                    xc, xk = Ft()
                    dma("sp", xc[:, 0:n], xT[k * 128:(k + 1) * 128, o:o + n], [("xT", k)], [xk], "xc%d" % (st["f"] % NF))
                    sq, sk = Ht()
                    tt(sq[:, 0:n], xc[:, 0:n], xc[:, 0:n], ALU.mult, [xk], [sk])
                    mm(accs[bi][0][:, 0:n], ones[:, :], sq[:, 0:n], k == 0, k == KC - 1, [sk, ("ones",)], [accs[bi][1]])
            for bi, (o, n) in enumerate([(0, 384), (384, 384), (768, 384)]):
                rstd_from(accs[bi][0][:, 0:n], accs[bi][1], n, D, rstd_b[:, o:o + n], ("rstd_b", bi))

        RSB = [("rstd_b", i) for i in range(3)]

        def phase1(l):
            set_gpool([0, 1, 2, 3, 4])
            if l == 0 or nph < 4:
                rms_stats(l)
            xT_v3 = xT.ap().rearrange("(k p) t -> p k t", p=128)
            groups = [(bi, o, n, kg) for bi, (o, n) in enumerate(OWN_BLOCKS) for kg in range(4)]
            loaded = {}

            def issue(i):
                bi, o, n, kg = groups[i]
                fg = i % 3
                keys = [("F", fg * 4 + j) for j in range(4)]
                dma("sp", Fp[:, fg * 4:(fg + 1) * 4, 0:n], xT_v3[:, kg * 4:(kg + 1) * 4, o:o + n],
                    [("xT", kg * 4 + j) for j in range(4)], keys, "xg%d" % fg)
                loaded[i] = (fg, keys)

            issue(0)
            issue(1)
            for i, (bi, o, n, kg) in enumerate(groups):
                if i + 2 < len(groups):
                    issue(i + 2)
                if kg == 0 and l + 1 < depth:
                    for cg in range(4 * bi, 4 * bi + 4):
                        ada_group(l + 1, cg)
                fg, keys = loaded.pop(i)
                hs = i % 2
                hview = Hp[:, 0:4, :] if hs == 0 else hst[:, :, :]
                hkeys = [("H", j) for j in range(4)] if hs == 0 else [("hst", j) for j in range(4)]
                col = 1 if o == 0 else 0
                for j in range(4):
                    k = kg * 4 + j
                    xc = Fp[:, fg * 4 + j, :]
                    tt(xc[:, 0:n], xc[:, 0:n], rstd_b[:, o:o + n], ALU.mult, [keys[j]] + RSB, [keys[j]])
                    act(hview[:, j, 0:n], xc[:, 0:n], AF.Identity, [keys[j], ("gs", l), ("mod", l)], [hkeys[j]],
                        scale=gs[:, l, k, col:col + 1], bias=mod[:, l, k, col:col + 1])
                dma("sp", hx_in[bi].ap().rearrange("(k p) t -> p k t", p=128)[:, kg * 4:(kg + 1) * 4, :], hview[:, :, 0:n],
                    hkeys, [("hx_in", bi)], "hxst%d" % bi)
                if kg == 3:
                    P.add("pool", lambda e, bi=bi: e.collective_compute("AllGather", ALU.bypass,
                                                                        replica_groups=[[0, 1], [2, 3], [4, 5], [6, 7]],
                                                                        ins=[hx_in[bi].ap().opt()], outs=[hx_all[bi].ap().opt()]),
                          [("hx_in", bi)], [("hx_all", bi)], chan="cc", inc=1)
            if l + 1 < depth:
                ada_gs(l + 1)

        def load_hblock(r, o, n):
            i = st["s"] % 2
            st["s"] += 1
            key = P.akey("A", ("hb", i))
            bi = [b[0] for b in OWN_BLOCKS].index(o)
            src = hx_all[bi].ap().rearrange("(r k p) t -> r p k t", r=2, p=128)[r]
            dma("sp", hb_v[i][:, :, 0:n], src[:, :, 0:n], [("hx_all", bi)], [key], "hb%d" % i)
            return hb_v[i], key

        def load_tab(src, pair, o, n):
            tb, tk, tchan = tab_slot()
            dma("sp", tb[:, :, 0:n], src[:, 2 * pair:2 * pair + 2, o:o + n], [], [tk], tchan)
            return tb, tk

        def phase2(l):
            set_gpool([0, 1, 2, 3, 4, 5])
            KGK = P.akey("C", "Kg")
            VGK = P.akey("C", "Vg")
            slot, wkey, wchan = wa_slot()
            dma("pool", slot[:, :, :], w_in[l][:, C_KV:C_KV + 512].rearrange("(k p) c -> p k c", p=128), [], [wkey], wchan)
            for (r, o, n) in KEY_BLOCKS:
                k0 = r * TO + o
                hb, hk = load_hblock(r, o, n)
                tb, tk = load_tab(tabk, 0, k0, n)
                for c in range(2):
                    pb, pk = Gb()
                    for kc in range(KC):
                        mm(pb[:, 0:n], slot[:, kc, c * 128:(c + 1) * 128], hb[:, kc, 0:n], kc == 0, kc == KC - 1, [wkey, hk], [pk])
                    normrope(pb[:, 0:n], pk, n, 128, tb, tk, KgT[:, c, k0:k0 + n], [KGK],
                             gain=lc(l, 66), gain_sw=lc(l, 67), norm=True)
                    tick()
                for tti in range(n // 128):
                    kt = (k0 // 128) + tti
                    pb, pk = Gb()
                    for kc in range(KC):
                        mm(pb[:, 0:256], hb[:, kc, tti * 128:(tti + 1) * 128], slot[:, kc, 256:512], kc == 0, kc == KC - 1, [wkey, hk], [pk])
                    act(Vg[:, kt, :], pb[:, 0:256], AF.Copy, [pk], [VGK])
                    tick()
            slot, wkey, wchan = wa_slot()
            dma("pool", slot[:, :, :], w_in[l][:, C_CKV:C_CKV + 512].rearrange("(k p) c -> p k c", p=128), [], [wkey], wchan)
            for (r, o, n) in KEY_BLOCKS:
                k0 = r * TO + o
                hb, hk = load_hblock(r, o, n)
                norm4(lambda kc: hb[:, kc, 0:n], [hk], n, slot, wkey, 72, l,
                      lambda c: ckvn[:, c, k0:k0 + n], [("ckvn",)])
            slot, wkey, wchan = wa_slot()
            dma("pool", slot[:, :, 0:64], w_in[l][:, C_KR:C_KR + 64].rearrange("(k p) c -> p k c", p=128), [], [wkey], wchan)
            for (r, o, n) in KEY_BLOCKS:
                k0 = r * TO + o
                hb, hk = load_hblock(r, o, n)
                tb, tk = load_tab(tabk, 1, k0, n)
                pb, pk = Gb()
                for kc in range(KC):
                    mm(pb[0:64, 0:n], slot[:, kc, 0:64], hb[:, kc, 0:n], kc == 0, kc == KC - 1, [wkey, hk], [pk])
                normrope(pb[0:64, 0:n], pk, n, 64, tb, tk, krT[0:64, k0:k0 + n], [("krT",)])
                tick()
            flush()

        def gate_proj(l, slot, wkey, coff, HTK, sgi):
            for (o, n) in OWN_BLOCKS:
                pb, pk = Gb()
                for kc in range(KC):
                    mm(pb[:, 0:n], slot[:, kc, coff:coff + 128], hTo[:, kc, o:o + n], kc == 0, kc == KC - 1, [wkey, HTK], [pk])
                e1, ek = Ft()
                act(e1[:, 0:n], pb[:, 0:n], AF.Exp, [pk], [ek], scale=-1.0)
                ts(e1[:, 0:n], e1[:, 0:n], 1.0, None, ALU.add, None, [ek], [ek])
                recip(e1[:, 0:n], e1[:, 0:n], [ek], [ek])
                tt(sg[sgi][:, o:o + n], pb[:, 0:n], e1[:, 0:n], ALU.mult, [pk, ek], [("sg", sgi, o)])
                tick()

        S_BANKS = [2, 3, 6, 7]
        LOOK = 3

        def attention(l, sgi, q1, q1key, k1_fn, k1key, v_fn, vkey, scale, mla, qr=None, qrkey=None):
            items = []
            for (o, n, kts) in [(128, 512, list(range(18))), (640, 512, list(range(18))), (0, 128, [0, 9])]:
                for i, kt in enumerate(kts):
                    items.append((o, n, kt, i, i == len(kts) - 1))
            live = {}
            cur = {}
            for idx in range(len(items) + LOOK):
                if idx < len(items):
                    o, n, kt, ii, last = items[idx]
                    si = S_BANKS[st["s"] % 4]
                    st["s"] += 1
                    Sb, Sk = ps[si], ("ps", si)
                    mm(Sb[:, 0:n], k1_fn(kt), q1[:, o:o + n], True, not mla, [k1key, q1key], [Sk])
                    if mla:
                        mm(Sb[:, 0:n], krT[:, kt * 128:(kt + 1) * 128], qr[:, o:o + n], False, True, [("krT",), qrkey], [Sk])
                    pt, ptk = Ht()
                    act(pt[:, 0:n], Sb[:, 0:n], AF.Exp, [Sk], [ptk], scale=scale)
                    live[idx] = (pt, ptk)
                j = idx - LOOK
                if j < 0:
                    continue
                o, n, kt, ii, last = items[j]
                first = ii == 0
                pt, ptk = live.pop(j)
                if first:
                    oi = st["o"] % 2
                    st["o"] += 1
                    cur = dict(Ob=ps[4 + oi], Ok=("ps", 4 + oi), acc=accs[oi], acck=("acc", oi), accd=accd[oi], accdk=("accd", oi))
                Ob, Ok, acc, acck, acd, acdk = cur["Ob"], cur["Ok"], cur["acc"], cur["acck"], cur["accd"], cur["accdk"]
                mm(Ob[:, 0:n], v_fn(kt), pt[:, 0:n], first, last, [vkey, ptk], [Ok])
                if ii == 0:
                    pcopy(acc[:, 0:n], pt[:, 0:n], [ptk], [acck])
                elif ii == 1:
                    vcopy(acd[:, 0:n], pt[:, 0:n], [ptk], [acdk])
                elif ii % 2 == 0:
                    ptt(acc[:, 0:n], acc[:, 0:n], pt[:, 0:n], ALU.add, [acck, ptk], [acck])
                else:
                    tt(acd[:, 0:n], acd[:, 0:n], pt[:, 0:n], ALU.add, [acdk, ptk], [acdk])
                if last:
                    Lb, Lk = Gb()
                    mm(Lb[:, 0:n], onesf[:, :], acc[:, 0:n], True, False, [acck, ("onesf",)], [Lk])
                    mm(Lb[:, 0:n], onesf[:, :], acd[:, 0:n], False, True, [acdk, ("onesf",)], [Lk])
                    ri, rk = Ft()
                    recip(ri[:, 0:n], Lb[:, 0:n], [Lk], [rk])
                    tt(ri[:, 0:n], Ob[:, 0:n], ri[:, 0:n], ALU.mult, [Ok, rk], [rk])
                    tt(sg[sgi][:, o:o + n], ri[:, 0:n], sg[sgi][:, o:o + n], ALU.mult, [rk, ("sg", sgi, o)], [("sg", sgi, o)])

        def phase3(l):
            P.fence("A", scr[:, 0:1])
            HTK = P.akey("A", "hTo")
            set_gpool([0, 1, 2, 3, 4, 5])
            for bi, (o, n) in enumerate(OWN_BLOCKS):
                dma("sp", hTo[:, :, o:o + n], hx_in[bi].ap().rearrange("(k p) t -> p k t", p=128), [("hx_in", bi)], [HTK], "hTo")
            slot, wkey, wchan = wa_slot()
            dma("pool", slot[:, :, :], w_in[l][:, C_CQ:C_CQ + 512].rearrange("(k p) c -> p k c", p=128), [], [wkey], wchan)
            for (o, n) in OWN_BLOCKS:
                norm4(lambda kc: hTo[:, kc, o:o + n], [HTK], n, slot, wkey, 68, l,
                      lambda c: cqn[:, c, o:o + n], [("cqn",)])
            set_gpool([0, 1])
            KGK = P.akey("C", "Kg")
            VGK = P.akey("C", "Vg")

            def prep_gqa(h):
                b = h % 2
                slot, wkey, wchan = wa_slot()
                dma("pool", slot[:, :, 0:256], w_in[l][:, C_QG + h * 256:C_QG + (h + 1) * 256].rearrange("(k p) c -> p k c", p=128),
                    [], [wkey], wchan)
                for (o, n) in OWN_BLOCKS:
                    tb, tk = load_tab(tabq, 0, o, n)
                    pb, pk = Gb()
                    for kc in range(KC):
                        mm(pb[:, 0:n], slot[:, kc, 0:128], hTo[:, kc, o:o + n], kc == 0, kc == KC - 1, [wkey, HTK], [pk])
                    normrope(pb[:, 0:n], pk, n, 128, tb, tk, Qg[b][:, o:o + n], [("Qg", b)],
                             gain=lc(l, 64), gain_sw=lc(l, 65), norm=True)
                    tick()
                gate_proj(l, slot, wkey, 128, HTK, b)
                flush()

            def attn_gqa(h):
                b = h % 2
                kvh = h // 4
                attention(l, b, Qg[b], ("Qg", b), lambda kt: KgT[:, kvh, kt * 128:(kt + 1) * 128], KGK,
                          lambda kt: Vg[:, kt, kvh * 128:(kvh + 1) * 128], VGK, 128.0 ** -0.5, False)
                dma("sp", yT[h * 128:(h + 1) * 128, :], sg[b][:, :], [("sg", b, o) for (o, n) in OWN_BLOCKS], [("yT", h)], "yst")

            prep_gqa(0)
            for h in range(8):
                if h + 1 < 8:
                    prep_gqa(h + 1)
                attn_gqa(h)

            P.fence("C", scr[:, 1:2])

            def prep_mla(h):
                b = h % 2
                KNK = P.akey("C", ("Kn", b))
                VMK = P.akey("C", ("Vm", b))
                wb, wbk = WB[b], ("WB", b)
                dma("pool", wb[:, :, :], w_mla[l, h].rearrange("(k p) c -> p k c", p=128), [], [wbk], "WB%d" % b)
                slot, wkey, wchan = wa_slot()
                dma("pool", slot[:, :, 0:128], w_in[l][:, C_GM + h * 128:C_GM + (h + 1) * 128].rearrange("(k p) c -> p k c", p=128),
                    [], [wkey], wchan)
                for (k0, n) in [(0, 512), (512, 512), (1024, 512), (1536, 512), (2048, 256)]:
                    pb, pk = Gb()
                    for kc in range(4):
                        mm(pb[:, 0:n], wb[:, kc, 192:320], ckvn[:, kc, k0:k0 + n], kc == 0, kc == 3, [wbk, ("ckvn",)], [pk])
                    act(Kn[b][:, k0:k0 + n], pb[:, 0:n], AF.Copy, [pk], [KNK])
                    tick()
                for g0 in range(0, 18, 4):
                    ng = min(4, 18 - g0)
                    pb, pk = Gb()
                    for j in range(ng):
                        kt = g0 + j
                        for kc in range(4):
                            mm(pb[:, j * 128:(j + 1) * 128], ckvn[:, kc, kt * 128:(kt + 1) * 128], wb[:, kc, 320:448],
                               kc == 0, kc == 3, [wbk, ("ckvn",)], [pk])
                    act(Vm[b][:, g0:g0 + ng, :], pb[:, 0:ng * 128].rearrange("p (a c) -> p a c", a=ng), AF.Copy, [pk], [VMK])
                    tick()
                for (o, n) in OWN_BLOCKS:
                    pb, pk = Gb()
                    for kc in range(4):
                        mm(pb[:, 0:n], wb[:, kc, 0:128], cqn[:, kc, o:o + n], kc == 0, kc == 3, [wbk, ("cqn",)], [pk])
                    act(Qn[b][:, o:o + n], pb[:, 0:n], AF.Copy, [pk], [("Qn", b)])
                    tick()
                    tb, tk = load_tab(tabq, 1, o, n)
                    pb, pk = Gb()
                    for kc in range(4):
                        mm(pb[0:64, 0:n], wb[:, kc, 128:192], cqn[:, kc, o:o + n], kc == 0, kc == 3, [wbk, ("cqn",)], [pk])
                    normrope(pb[0:64, 0:n], pk, n, 64, tb, tk, Qr[b][0:64, o:o + n], [("Qr", b)])
                    tick()
                gate_proj(l, slot, wkey, 0, HTK, b)
                flush()

            def attn_mla(h):
                b = h % 2
                KNK = P.akey("C", ("Kn", b))
                VMK = P.akey("C", ("Vm", b))
                attention(l, b, Qn[b], ("Qn", b), lambda kt: Kn[b][:, kt * 128:(kt + 1) * 128], KNK,
                          lambda kt: Vm[b][:, kt, :], VMK, 192.0 ** -0.5, True, qr=Qr[b], qrkey=("Qr", b))
                dma("sp", yT[(8 + h) * 128:(9 + h) * 128, :], sg[b][:, :], [("sg", b, o) for (o, n) in OWN_BLOCKS],
                    [("yT", 8 + h)], "yst")

            prep_mla(0)
            for h in range(8):
                if h + 1 < 8:
                    prep_mla(h + 1)
                attn_mla(h)
            P.fence("C", scr[:, 2:3])

        def phase4(l):
            P.fence("A", scr[:, 3:4])
            YK = P.akey("A", "yT")
            set_gpool([0, 1, 2, 3, 4])
            accb = [(ps[5], ("ps", 5)), (ps[6], ("ps", 6)), (ps[7], ("ps", 7))]
            dma("sp", hTo[:, :, :], yT.ap().rearrange("(k p) t -> p k t", p=128), [("yT", k) for k in range(KC)], [YK], "hTo")
            items = [(dc, bi, o, n) for dc in range(KC) for bi, (o, n) in enumerate(OWN_BLOCKS)]
            loaded = {}

            def issue(i):
                dc, bi, o, n = items[i]
                xb, xk = Ft()
                dma("sp", xb[:, 0:n], xT[dc * 128:(dc + 1) * 128, o:o + n], [("xT", dc)], [xk], "xc%d" % (st["f"] % NF))
                loaded[i] = (xb, xk)

            for i in range(3):
                issue(i)
            slot = wkey = None
            for i, (dc, bi, o, n) in enumerate(items):
                g, c = divmod(dc, 4)
                if c == 0 and bi == 0:
                    slot, wkey, wchan = wa_slot()
                    dma("pool", slot[:, :, :], w_out[l][:, g * 512:(g + 1) * 512].rearrange("(k p) c -> p k c", p=128), [], [wkey], wchan)
                if i + 3 < len(items):
                    issue(i + 3)
                col = 1 if o == 0 else 0
                xb, xk = loaded.pop(i)
                pb, pk = Gb()
                for kc in range(KC):
                    mm(pb[:, 0:n], slot[:, kc, c * 128:(c + 1) * 128], hTo[:, kc, o:o + n], kc == 0, kc == KC - 1, [wkey, YK], [pk])
                stt(xb[:, 0:n], pb[:, 0:n], mod[:, l, 32 + dc, col:col + 1], xb[:, 0:n], ALU.mult, ALU.add,
                    [pk, xk, ("mod", l)], [xk])
                tick()
                dma("sp", xT[dc * 128:(dc + 1) * 128, o:o + n], xb[:, 0:n], [xk], [("xT", dc)], "xTst")
                sq, sk = Ht()
                tt(sq[:, 0:n], xb[:, 0:n], xb[:, 0:n], ALU.mult, [xk], [sk])
                defer(lambda sq=sq, sk=sk, bi=bi, n=n, dc=dc: mm(accb[bi][0][:, 0:n], ones[:, :], sq[:, 0:n], dc == 0, dc == KC - 1,
                                                                 [sk, ("ones",)], [accb[bi][1]]))
            flush()
            for bi, (o, n) in enumerate(OWN_BLOCKS):
                rstd_from(accb[bi][0][:, 0:n], accb[bi][1], n, D, rstd_b[:, o:o + n], ("rstd_b", bi))
            P.fence("A", scr[:, 4:5])

        for l in range(depth):
            if nph >= 1:
                phase1(l)
            if nph >= 2:
                phase2(l)
            if nph >= 3:
                phase3(l)
            if nph >= 4:
                phase4(l)

        set_gpool([0, 1, 2, 3, 4])
        if depth == 0 or nph < 4:
            rms_stats(depth)
        out_v = out
        for t in range(8):
            tok0 = 128 + t * 128
            xs_keys = [("F", 4 + i) for i in range(4)]
            xo_keys = [("F", 8 + i) for i in range(4)]
            dma("sp", Fp[:, 4:8, :].rearrange("p a (j t) -> p (a j) t", j=4), xT_v[:, :, tok0:tok0 + 128],
                [("xT", k) for k in range(KC)], xs_keys, "xs")
            for kg in range(4):
                for j in range(4):
                    k = kg * 4 + j
                    stt(Fp[:, 4 + kg, j * 128:(j + 1) * 128], Fp[:, 4 + kg, j * 128:(j + 1) * 128],
                        cst[:, CC_FG + k:CC_FG + k + 1], rstd_b[:, tok0:tok0 + 128], ALU.mult, ALU.mult,
                        [xs_keys[kg], CST] + RSB, [xs_keys[kg]])
                pb, pk = Gb()
                for j in range(4):
                    tr(pb[:, j * 128:(j + 1) * 128], Fp[:, 4 + kg, j * 128:(j + 1) * 128], ident, [xs_keys[kg], CST], [pk])
                act(Fp[:, 8 + kg, :], pb[:, :], AF.Copy, [pk], [xo_keys[kg]])
            dma("sp", out_v[t * 128:(t + 1) * 128, :].rearrange("t (a b) -> t a b", a=4), Fp[:, 8:12, :], xo_keys, [("out", t)], "outst")
        if dbg_out is not None:
            if dbg == "hx":
                dma("pool", dbg_out[:, 128:640], hx_all[1][D:2 * D, :], [("hx_all", 1)], [("dbg",)], "dbgst")
            elif dbg == "y":
                dma("pool", dbg_out[:, :], yT.ap(), [("yT", k) for k in range(KC)], [("dbg",)], "dbgst")
            else:
                dma("sp", dbg_out[:, :], xT.ap(), [("xT", k) for k in range(KC)], [("dbg",)], "dbgst")
        print('sbuf bytes remaining', nc.sbuf_bytes_remaining)
        P.finalize(nc, es)
        with nc.Block() as block:
            @block.tensor
            def _(e):
                P.emit("pe", e)

            @block.scalar
            def _(e):
                P.emit("act", e)

            @block.vector
            def _(e):
                P.emit("dve", e)

            @block.gpsimd
            def _(e):
                P.emit("pool", e)

            @block.sync
            def _(e):
                P.emit("sp", e)
                e.wait_ge(P.sems[("chan", "outst")], P.chan_hist["outst"][-1][1])
                if ("chan", "dbgst") in P.sems:
                    e.wait_ge(P.sems[("chan", "dbgst")], P.chan_hist["dbgst"][-1][1])
    return nc


def _rope_tables(rot_dim):
    rows = SEQ // GRID_W
    row = np.repeat(np.arange(rows, dtype=np.float32), GRID_W)
    col = np.tile(np.arange(GRID_W, dtype=np.float32), rows)
    n_freq = rot_dim // 4
    inv = (np.float32(10000.0) ** (-np.arange(n_freq, dtype=np.float32) / np.float32(n_freq))).astype(np.float32)
    ang = np.concatenate([row[:, None] * inv[None], col[:, None] * inv[None]], axis=-1).astype(np.float32)
    return np.cos(ang).astype(np.float32), np.sin(ang).astype(np.float32)


def _feature_major_tables(rot_dim):
    cos, sin = _rope_tables(rot_dim)
    half = rot_dim // 2
    C = np.concatenate([cos.T, cos.T], axis=0)
    S = np.concatenate([-sin.T, sin.T], axis=0)
    assert C.shape == (rot_dim, SEQ) and half * 2 == rot_dim
    return C.astype(np.float32), S.astype(np.float32)


def _own_table(C, S, half):
    F = C.shape[0]
    c = np.concatenate([np.ones((F, 128), np.float32), C[:, half * 1024:(half + 1) * 1024]], axis=1)
    s = np.concatenate([np.zeros((F, 128), np.float32), S[:, half * 1024:(half + 1) * 1024]], axis=1)
    return c, s


def _prepare(x, c, ctx, c_ctx, w_ada, b_ada, norm_g, w_in, q_gain, k_gain, cq_gain, ckv_gain, w_uq, w_ukv, w_out, final_g):
    f = np.float32
    x = np.asarray(x, f); c = np.asarray(c, f); ctx = np.asarray(ctx, f); c_ctx = np.asarray(c_ctx, f)
    w_ada = np.ascontiguousarray(np.asarray(w_ada, f)); b_ada = np.asarray(b_ada, f); norm_g = np.asarray(norm_g, f)
    w_in = np.asarray(w_in, f); q_gain = np.asarray(q_gain, f); k_gain = np.asarray(k_gain, f)
    cq_gain = np.asarray(cq_gain, f); ckv_gain = np.asarray(ckv_gain, f)
    w_uq = np.asarray(w_uq, f); w_ukv = np.asarray(w_ukv, f); w_out = np.ascontiguousarray(np.asarray(w_out, f))
    final_g = np.asarray(final_g, f)
    q0, k0, v0, g0, cq0, ckv0, kr0, gm0 = 0, 1024, 1280, 1536, 2560, 3072, 3584, 3648
    cols = []
    for h in range(8):
        cols += list(range(q0 + h * 128, q0 + (h + 1) * 128)) + list(range(g0 + h * 128, g0 + (h + 1) * 128))
    cols += list(range(cq0, cq0 + 512))
    cols += list(range(gm0, gm0 + 1024))
    cols += list(range(k0, k0 + 256)) + list(range(v0, v0 + 256)) + list(range(ckv0, ckv0 + 512)) + list(range(kr0, kr0 + 64))
    assert len(cols) == 4672
    w_in_p = np.ascontiguousarray(w_in[:, :, cols])
    w_mla_p = np.empty((DEPTH, 8, 512, 448), f)
    for h in range(8):
        w_mla_p[:, h, :, 0:192] = w_uq[:, :, h * 192:(h + 1) * 192]
        w_mla_p[:, h, :, 192:448] = w_ukv[:, :, h * 256:(h + 1) * 256]
    C128, S128 = _feature_major_tables(128)
    C64, S64 = _feature_major_tables(64)
    own = []
    for half in range(2):
        t = np.zeros((128, 4, TO), f)
        t[:, 0], t[:, 1] = _own_table(C128, S128, half)
        t[0:64, 2], t[0:64, 3] = _own_table(C64, S64, half)
        own.append(t)
    tabk = np.ascontiguousarray(np.concatenate(own, axis=2))
    sw = (np.arange(128) + 64) % 128
    in_maps = []
    for core in range(8):
        b, half = core // 2, core % 2
        x_own = np.ascontiguousarray(np.concatenate([ctx[b, half * 128:(half + 1) * 128], x[b, half * 1024:(half + 1) * 1024]], axis=0))
        cst = np.zeros((128, NCC), f)
        cst[:, CC_ID:CC_ID + 128] = np.eye(128, dtype=f)
        for m in range(128):
            cst[(m + 64) % 128, CC_SW128 + m] = 1.0
        for m in range(64):
            cst[(m + 32) % 64, CC_SW64 + m] = 1.0
        for l in range(DEPTH):
            base = CC_L + l * CC_LW
            cst[:, base:base + 16] = norm_g[l].reshape(16, 128).T
            cst[:, base + 16:base + 64] = b_ada[l].reshape(48, 128).T
            cst[:, base + 64] = q_gain[l]
            cst[:, base + 65] = q_gain[l][sw]
            cst[:, base + 66] = k_gain[l]
            cst[:, base + 67] = k_gain[l][sw]
            cst[:, base + 68:base + 72] = cq_gain[l].reshape(4, 128).T
            cst[:, base + 72:base + 76] = ckv_gain[l].reshape(4, 128).T
        cst[:, CC_FG:CC_FG + 16] = final_g.reshape(16, 128).T
        cst[:, CC_CB:CC_CB + 16] = c[b].reshape(16, 128).T
        cst[:, CC_CC:CC_CC + 16] = c_ctx.reshape(16, 128).T
        in_maps.append({"x_own": x_own, "consts": cst, "tabq": own[half], "tabk": tabk, "w_ada": w_ada,
                        "w_in_p": w_in_p, "w_mla_p": w_mla_p, "w_out": w_out})
    return in_maps


def kernel(x, c, ctx, c_ctx, w_ada, b_ada, norm_g, w_in, q_gain, k_gain, cq_gain, ckv_gain, w_uq, w_ukv, w_out, final_g):
    in_maps = _prepare(x, c, ctx, c_ctx, w_ada, b_ada, norm_g, w_in, q_gain, k_gain, cq_gain, ckv_gain, w_uq, w_ukv, w_out, final_g)
    nc = build_program(DEPTH)
    res = run_bass_kernel_spmd(nc, in_maps, core_ids=list(range(8)))
    outp = np.empty((4, SEQ, D), np.float32)
    for core in range(8):
        b, half = core // 2, core % 2
        outp[b, half * 1024:(half + 1) * 1024] = np.asarray(res.results[core]["out"], np.float32)
    return outp
```

```python
import bisect
from contextlib import ExitStack

import numpy as np
import concourse.bass as bass
import concourse.mybir as mybir
from concourse.bass_utils import run_bass_kernel_spmd

F32 = mybir.dt.float32
BF16 = mybir.dt.bfloat16
ALU = mybir.AluOpType
AF = mybir.ActivationFunctionType

D = 2048
KC = 16
DEPTH = 4
TO = 1152
TA = 2304
SEQ = 2048
CTX = 256
GRID_W = 64
EPS = 1e-6
OWN_BLOCKS = [(0, 128), (128, 512), (640, 512)]
KEY_BLOCKS = [(r, o, n) for r in range(2) for (o, n) in OWN_BLOCKS]
C_QG = 0
C_CQ = 2048
C_GM = 2560
C_KV = 3584
C_CKV = 4096
C_KR = 4608
CC_ID = 0
CC_SW128 = 128
CC_SW64 = 256
CC_L = 320
CC_LW = 76
CC_FG = CC_L + DEPTH * CC_LW
CC_CB = CC_FG + 16
CC_CC = CC_CB + 16
NCC = CC_CC + 16
EPOCH = 8000


class Prog:
    def __init__(self):
        self.ops = []
        self.last_w = {}
        self.readers = {}
        self.chan_hist = {}
        self.arena_gen = {}
        self.arena_keys = {}
        self.arena_fence = {}

    def akey(self, arena, sub):
        g = self.arena_gen.setdefault(arena, 0)
        k = ("A", arena, g, sub)
        ks = self.arena_keys.setdefault((arena, g), set())
        if k not in ks:
            ks.add(k)
            f = self.arena_fence.get((arena, g))
            if f is not None and k not in self.last_w:
                self.last_w[k] = f
                self.readers[k] = []
        return k

    def add(self, eng, fn, r=(), w=(), chan=None, inc=16):
        oid = len(self.ops)
        deps = set()
        for k in r:
            lw = self.last_w.get(k)
            if lw is not None:
                deps.add(lw)
        for k in w:
            lw = self.last_w.get(k)
            if lw is not None:
                deps.add(lw)
            deps.update(self.readers.get(k, ()))
        deps.discard(oid)
        for k in r:
            self.readers.setdefault(k, []).append(oid)
        for k in w:
            self.last_w[k] = oid
            self.readers[k] = []
        self.ops.append(dict(eng=eng, fn=fn, deps=deps, chan=chan, inc=inc, signal=False))
        return oid

    def fence(self, arena, tile):
        g = self.arena_gen.setdefault(arena, 0)
        keys = list(self.arena_keys.get((arena, g), ()))
        oid = self.add("dve", lambda e: e.memset(tile, 0.0), r=(), w=keys + [("fence", arena)])
        self.arena_gen[arena] = g + 1
        self.arena_fence[(arena, g + 1)] = oid
        return oid

    def finalize(self, nc, es):
        ops = self.ops
        for o in ops:
            for d in o["deps"]:
                if ops[d]["eng"] == "pe" and o["eng"] == "pe" and ops[d]["chan"] is None:
                    continue
                ops[d]["signal"] = True
        cnt = {}
        self.sems = {}
        for i, o in enumerate(ops):
            if o["chan"] is not None:
                c = o["chan"]
                cum = (self.chan_hist[c][-1][1] if c in self.chan_hist else 0) + o["inc"]
                self.chan_hist.setdefault(c, []).append((i, cum))
                o["sig"] = (("chan", c), cum)
                if ("chan", c) not in self.sems:
                    self.sems[("chan", c)] = es.enter_context(nc.semaphore("c_" + c))
            elif o["signal"]:
                n = cnt.get(o["eng"], 0)
                ep, v = divmod(n, EPOCH)
                cnt[o["eng"]] = n + 1
                sk = ("eng", o["eng"], ep)
                if sk not in self.sems:
                    self.sems[sk] = es.enter_context(nc.semaphore("e_%s_%d" % (o["eng"], ep)))
                o["sig"] = (sk, v + 1)
        for i, o in enumerate(ops):
            waits = {}
            for d in o["deps"]:
                od = ops[d]
                if od["eng"] == "pe" and o["eng"] == "pe" and od["chan"] is None:
                    continue
                if od["chan"] is not None:
                    hist = self.chan_hist[od["chan"]]
                    j = bisect.bisect_left(hist, (i, -1)) - 1
                    sk, val = ("chan", od["chan"]), hist[j][1]
                else:
                    sk, val = od["sig"]
                if waits.get(sk, 0) < val:
                    waits[sk] = val
            o["waits"] = waits

    def emit(self, engname, e):
        seen = {}
        for o in self.ops:
            if o["eng"] != engname:
                continue
            for sk, val in o["waits"].items():
                if seen.get(sk, 0) >= val:
                    continue
                seen[sk] = val
                e.wait_ge(self.sems[sk], val)
            ins = o["fn"](e)
            if o["chan"] is not None:
                ins.then_inc(self.sems[o["sig"][0]], o["inc"])
            elif o["signal"]:
                ins.then_inc(self.sems[o["sig"][0]], 1)


def build_program(depth=DEPTH, dbg=None, wd=DEPTH, nph=4):
    nc = bass.Bass("TRN2", target_bir_lowering=False)
    x_in = nc.dram_tensor("x_own", [TO, D], F32, kind="ExternalInput").ap()
    cst_in = nc.dram_tensor("consts", [128, NCC], F32, kind="ExternalInput").ap()
    tabq = nc.dram_tensor("tabq", [128, 4, TO], F32, kind="ExternalInput").ap()
    tabk = nc.dram_tensor("tabk", [128, 4, TA], F32, kind="ExternalInput").ap()
    w_ada = nc.dram_tensor("w_ada", [wd, D, 3 * D], F32, kind="ExternalInput").ap()
    w_in = nc.dram_tensor("w_in_p", [wd, D, 4672], F32, kind="ExternalInput").ap()
    w_mla = nc.dram_tensor("w_mla_p", [wd, 8, 512, 448], F32, kind="ExternalInput").ap()
    w_out = nc.dram_tensor("w_out", [wd, D, D], F32, kind="ExternalInput").ap()
    out = nc.dram_tensor("out", [1024, D], F32, kind="ExternalOutput").ap()
    xT = nc.dram_tensor("xT_s", [D, TO], F32)
    hx_in = [nc.dram_tensor("hx_in%d" % i, [D, n], BF16) for i, (o, n) in enumerate(OWN_BLOCKS)]
    hx_all = [nc.dram_tensor("hx_all%d" % i, [2 * D, n], BF16) for i, (o, n) in enumerate(OWN_BLOCKS)]
    yT = nc.dram_tensor("yT_s", [D, TO], BF16)
    dbg_out = None
    if dbg is not None:
        dbg_out = nc.dram_tensor("dbg", [D, TO], F32, kind="ExternalOutput").ap()

    P = Prog()
    es = ExitStack()
    with es:
        def sb(name, shape, dt):
            return es.enter_context(nc.sbuf_tensor(name, shape, dt))

        arenaA = sb("arenaA", [128, KC * TO], BF16)
        arenaC = sb("arenaC", [128, 9216], BF16)
        ckvn = sb("ckvn", [128, 4, TA], BF16)
        krT = sb("krT", [128, TA], BF16)
        Qg = [sb("Qg%d" % i, [128, TO], BF16) for i in range(2)]
        Qn = [sb("Qn%d" % i, [128, TO], BF16) for i in range(2)]
        Qr = [sb("Qr%d" % i, [128, TO], BF16) for i in range(2)]
        sg = [sb("sg%d" % i, [128, TO], BF16) for i in range(2)]
        cqn = sb("cqn", [128, 4, TO], BF16)
        WA = [sb("WA%d" % i, [128, KC, 512], BF16) for i in range(2)]
        WB = [sb("WB%d" % i, [128, 4, 448], BF16) for i in range(2)]
        tabs = [sb("tab%d" % i, [128, 2, 512], F32) for i in range(2)]
        NF = 12
        Fp = sb("Fp", [128, NF, 512], F32)
        NH = 6
        Hp = sb("Hp", [128, NH, 512], BF16)
        hst = sb("hst", [128, 4, 512], BF16)
        rstd_b = sb("rstd_b", [128, TO], F32)
        accs = [sb("acc%d" % i, [128, 512], F32) for i in range(2)]
        accd = [sb("accd%d" % i, [128, 512], F32) for i in range(2)]
        cst = sb("cst", [128, NCC], F32)
        mod = sb("mod", [128, DEPTH, 48, 2], F32)
        gs = sb("gs", [128, DEPTH, 16, 2], F32)
        ones = sb("ones", [128, 128], BF16)
        onesf = sb("onesf", [128, 128], F32)
        scT = sb("scT", [128, 16, 2], BF16)
        mrow = sb("mrow", [2, 512], F32)
        scr = sb("scr", [128, 8], F32)
        ps = [es.enter_context(nc.psum_tensor("ps%d" % i, [128, 512], F32)) for i in range(8)]

        hb_v = [arenaA[:, i * 8192:(i + 1) * 8192].rearrange("p (k t) -> p k t", k=KC) for i in range(2)]
        hTo = arenaA[:, :].rearrange("p (k t) -> p k t", k=KC)
        KgT = arenaC[:, 0:4608].rearrange("p (c t) -> p c t", c=2)
        Vg = arenaC[:, 4608:9216].rearrange("p (t c) -> p t c", t=18)
        Kn = [arenaC[:, i * 2304:(i + 1) * 2304] for i in range(2)]
        Vm = [arenaC[:, 4608 + i * 2304:4608 + (i + 1) * 2304].rearrange("p (t c) -> p t c", t=18) for i in range(2)]

        ident = cst[:, CC_ID:CC_ID + 128]
        psw128 = cst[:, CC_SW128:CC_SW128 + 128]
        psw64 = cst[0:64, CC_SW64:CC_SW64 + 64]

        def lc(l, off, n=1):
            b = CC_L + l * CC_LW + off
            return cst[:, b:b + n]

        st = dict(f=0, h=0, g=0, tab=0, s=0, o=0, wa=0)

        def Ft():
            i = st["f"] % NF
            st["f"] += 1
            return Fp[:, i, :], ("F", i)

        def Ht():
            i = st["h"] % NH
            st["h"] += 1
            return Hp[:, i, :], ("H", i)

        gpool = [0, 1]

        def Gb():
            i = gpool[st["g"] % len(gpool)]
            st["g"] += 1
            return ps[i], ("ps", i)

        def set_gpool(lst):
            gpool[:] = lst
            st["g"] = 0

        def wa_slot():
            i = st["wa"] % 2
            st["wa"] += 1
            return WA[i], ("WA", i), "WA%d" % i

        def tab_slot():
            i = st["tab"] % 2
            st["tab"] += 1
            return tabs[i], ("tab", i), "tab%d" % i

        def mm(out_, lhsT, rhs, start, stop, r, w):
            P.add("pe", lambda e: e.matmul(out=out_, lhsT=lhsT, rhs=rhs, start=start, stop=stop), r, w)

        def tr(out_, in_, idn, r, w):
            P.add("pe", lambda e: e.transpose(out=out_, in_=in_, identity=idn), r, w)

        def act(out_, in_, func, r, w, scale=1.0, bias=0.0):
            P.add("act", lambda e: e.activation(out=out_, in_=in_, func=func, bias=bias, scale=scale), r, w)

        def tt(out_, in0, in1, op, r, w):
            P.add("dve", lambda e: e.tensor_tensor(out=out_, in0=in0, in1=in1, op=op), r, w)

        def ts(out_, in0, s1, s2, op0, op1, r, w):
            if op1 is None:
                P.add("dve", lambda e: e.tensor_scalar(out=out_, in0=in0, scalar1=s1, scalar2=0.0, op0=op0, op1=ALU.add), r, w)
            else:
                P.add("dve", lambda e: e.tensor_scalar(out=out_, in0=in0, scalar1=s1, scalar2=s2, op0=op0, op1=op1), r, w)

        def stt(out_, in0, scalar, in1, op0, op1, r, w):
            P.add("dve", lambda e: e.scalar_tensor_tensor(out=out_, in0=in0, scalar=scalar, in1=in1, op0=op0, op1=op1), r, w)

        def recip(out_, in_, r, w):
            P.add("dve", lambda e: e.reciprocal(out=out_, in_=in_), r, w)

        def vcopy(out_, in_, r, w):
            P.add("dve", lambda e: e.tensor_copy(out=out_, in_=in_), r, w)

        def dma(eng, out_, in_, r, w, chan):
            P.add(eng, lambda e: e.dma_start(out=out_, in_=in_), r, w, chan=chan)

        pend = []
        gc = [0]

        def defer(fn, lag=1):
            pend.append((gc[0] + lag, fn))

        def tick():
            gc[0] += 1
            run = [p for p in pend if p[0] <= gc[0]]
            pend[:] = [p for p in pend if p[0] > gc[0]]
            for _, fn in run:
                fn()

        def flush():
            while pend:
                tick()

        def ptt(out_, in0, in1, op, r, w):
            P.add("pool", lambda e: e.tensor_tensor(out=out_, in0=in0, in1=in1, op=op), r, w)

        def pcopy(out_, in_, r, w):
            P.add("pool", lambda e: e.tensor_copy(out=out_, in_=in_), r, w)

        def rstd_from(ss_ap, sskey, n, nfeat, out_ap, outkey):
            act(out_ap, ss_ap, AF.Sqrt, [sskey], [outkey], scale=1.0 / nfeat, bias=EPS)
            recip(out_ap, out_ap, [outkey], [outkey])

        CST = ("cst",)

        def normrope(psA, pskey, n, F, tab, tabkey, out_ap, outkeys, gain=None, gain_sw=None, norm=False):
            raw, rk = Ft()
            act(raw[0:F, 0:n], psA, AF.Copy, [pskey], [rk])
            sq, sk = (None, None)
            if norm:
                sq, sk = Ht()
                tt(sq[0:F, 0:n], raw[0:F, 0:n], raw[0:F, 0:n], ALU.mult, [rk], [sk])

            def tail():
                if norm:
                    ssb, ssk = Gb()
                    mm(ssb[0:F, 0:n], ones[0:F, 0:F], sq[0:F, 0:n], True, True, [sk, ("ones",)], [ssk])
                    rs, rsk = Ft()
                    rstd_from(ssb[0:F, 0:n], ssk, n, F, rs[0:F, 0:n], rsk)
                swb, swk = Gb()
                pw = psw128 if F == 128 else psw64
                mm(swb[0:F, 0:n], pw, raw[0:F, 0:n], True, True, [rk, CST], [swk])
                t1, t1k = Ft()
                t2, t2k = Ft()
                if gain is not None:
                    stt(t1[0:F, 0:n], raw[0:F, 0:n], gain, tab[0:F, 0, 0:n], ALU.mult, ALU.mult, [rk, tabkey, CST], [t1k])
                    stt(t2[0:F, 0:n], swb[0:F, 0:n], gain_sw, tab[0:F, 1, 0:n], ALU.mult, ALU.mult, [swk, tabkey, CST], [t2k])
                else:
                    tt(t1[0:F, 0:n], raw[0:F, 0:n], tab[0:F, 0, 0:n], ALU.mult, [rk, tabkey], [t1k])
                    tt(t2[0:F, 0:n], swb[0:F, 0:n], tab[0:F, 1, 0:n], ALU.mult, [swk, tabkey], [t2k])
                if norm:
                    tt(t1[0:F, 0:n], t1[0:F, 0:n], t2[0:F, 0:n], ALU.add, [t1k, t2k], [t1k])
                    tt(out_ap, t1[0:F, 0:n], rs[0:F, 0:n], ALU.mult, [t1k, rsk], outkeys)
                else:
                    tt(out_ap, t1[0:F, 0:n], t2[0:F, 0:n], ALU.add, [t1k, t2k], outkeys)
            defer(tail, 2)

        def norm4(src_fn, srckeys, n, wslot, wkey, gain_off, l, out_fn, outkeys):
            raws = []
            ssb, ssk = ps[6], ("ps", 6)
            for c in range(4):
                pb, pk = Gb()
                for kc in range(KC):
                    mm(pb[:, 0:n], wslot[:, kc, c * 128:(c + 1) * 128], src_fn(kc), kc == 0, kc == KC - 1,
                       [wkey] + srckeys, [pk])
                raw, rk = Ft()
                act(raw[:, 0:n], pb[:, 0:n], AF.Copy, [pk], [rk])
                sq, sk = Ht()
                tt(sq[:, 0:n], raw[:, 0:n], raw[:, 0:n], ALU.mult, [rk], [sk])
                defer(lambda sq=sq, sk=sk, c=c: mm(ssb[:, 0:n], ones[:, :], sq[:, 0:n], c == 0, c == 3, [sk, ("ones",)], [ssk]), 2)
                tick()
                raws.append((raw, rk))
            flush()
            rs, rsk = Ft()
            rstd_from(ssb[:, 0:n], ssk, n, 512, rs[:, 0:n], rsk)
            for c in range(4):
                raw, rk = raws[c]
                stt(out_fn(c), raw[:, 0:n], lc(l, gain_off + c), rs[:, 0:n], ALU.mult, ALU.mult, [rk, rsk, CST], outkeys)

        dma("sp", cst[:, :], cst_in[:, :], [], [CST], "cst")
        P.add("dve", lambda e: e.memset(ones[:, :], 1.0), [], [("ones",)])
        P.add("dve", lambda e: e.memset(onesf[:, :], 1.0), [], [("onesf",)])
        P.add("dve", lambda e: e.memset(krT[:, :], 0.0), [], [("krT",)])
        for i in range(2):
            P.add("dve", lambda e, i=i: e.memset(Qr[i][:, :], 0.0), [], [("Qr", i)])
        f0, f0k = Ft()
        act(f0[:, 0:32], cst[:, CC_CB:CC_CB + 32], AF.Exp, [CST], [f0k], scale=-1.0)
        ts(f0[:, 0:32], f0[:, 0:32], 1.0, None, ALU.add, None, [f0k], [f0k])
        recip(f0[:, 0:32], f0[:, 0:32], [f0k], [f0k])
        for j in range(2):
            tt(scT[:, :, j], f0[:, j * 16:(j + 1) * 16], cst[:, CC_CB + j * 16:CC_CB + (j + 1) * 16], ALU.mult,
               [f0k, CST], [("scT",)])

        def ada_group(l, cg):
            slot, wkey, wchan = wa_slot()
            dma("pool", slot[:, :, :], w_ada[l][:, cg * 512:(cg + 1) * 512].rearrange("(k p) c -> p k c", p=128),
                [], [wkey], wchan)
            pb, pk = Gb()
            for kc in range(KC):
                mm(pb[0:2, 0:512], scT[:, kc, :], slot[:, kc, :], kc == 0, kc == KC - 1, [wkey, ("scT",)], [pk])
            act(mrow[0:2, 0:512], pb[0:2, 0:512], AF.Copy, [pk], [("mrow",)])
            for j in range(4):
                ch = cg * 4 + j
                pb2, pk2 = Gb()
                tr(pb2[:, 0:2], mrow[0:2, j * 128:(j + 1) * 128], ident[0:2, 0:2], [("mrow",), CST], [pk2])
                ts(mod[:, l, ch, :], pb2[:, 0:2], lc(l, 16 + ch), None, ALU.add, None, [pk2, CST], [("mod", l)])

        def ada_gs(l):
            for k in range(KC):
                ts(gs[:, l, k, :], mod[:, l, 16 + k, :], 1.0, lc(l, k), ALU.add, ALU.mult, [("mod", l), CST], [("gs", l)])

        set_gpool([0, 1, 2, 3, 4, 5])
        xT_v = xT.ap().rearrange("(k p) t -> p k t", p=128)
        for t in range(9):
            xs_keys = [("F", 4 + i) for i in range(4)]
            xo_keys = [("F", 8 + i) for i in range(4)]
            dma("sp", Fp[:, 4:8, :], x_in[t * 128:(t + 1) * 128, :].rearrange("t (a b) -> t a b", a=4), [], xs_keys, "xs")
            for kg in range(4):
                pb, pk = Gb()
                for j in range(4):
                    k = kg * 4 + j
                    tr(pb[:, j * 128:(j + 1) * 128], Fp[:, 4 + kg, j * 128:(j + 1) * 128], ident, xs_keys + [CST], [pk])
                act(Fp[:, 8 + kg, :], pb[:, :], AF.Copy, [pk], [xo_keys[kg]])
            dma("sp", xT_v[:, :, t * 128:(t + 1) * 128],
                Fp[:, 8:12, :].rearrange("p a (j t) -> p (a j) t", j=4), xo_keys, [("xT", k) for k in range(KC)], "xTst")
            if depth >= 1:
                ada_group(0, t)

        if depth >= 1:
            for cg in range(9, 12):
                ada_group(0, cg)
            ada_gs(0)

        def rms_stats(l):
            accs = [(ps[5], ("ps", 5)), (ps[6], ("ps", 6)), (ps[7], ("ps", 7))]
            for k in range(KC):
                for bi, (o, n) in enumerate([(0, 384), (384, 384), (768, 384)]):
                    xc, xk = Ft()
                    dma("sp", xc[:, 0:n], xT[k * 128:(k + 1) * 128, o:o + n], [("xT", k)], [xk], "xc%d" % (st["f"] % NF))
                    sq, sk = Ht()
                    tt(sq[:, 0:n], xc[:, 0:n], xc[:, 0:n], ALU.mult, [xk], [sk])
                    mm(accs[bi][0][:, 0:n], ones[:, :], sq[:, 0:n], k == 0, k == KC - 1, [sk, ("ones",)], [accs[bi][1]])
            for bi, (o, n) in enumerate([(0, 384), (384, 384), (768, 384)]):
                rstd_from(accs[bi][0][:, 0:n], accs[bi][1], n, D, rstd_b[:, o:o + n], ("rstd_b", bi))

        RSB = [("rstd_b", i) for i in range(3)]

        def phase1(l):
            set_gpool([0, 1, 2, 3, 4])
            if l == 0 or nph < 4:
                rms_stats(l)
            xT_v3 = xT.ap().rearrange("(k p) t -> p k t", p=128)
            groups = [(bi, o, n, kg) for bi, (o, n) in enumerate(OWN_BLOCKS) for kg in range(4)]
            loaded = {}

            def issue(i):
                bi, o, n, kg = groups[i]
                fg = i % 3
                keys = [("F", fg * 4 + j) for j in range(4)]
                dma("sp", Fp[:, fg * 4:(fg + 1) * 4, 0:n], xT_v3[:, kg * 4:(kg + 1) * 4, o:o + n],
                    [("xT", kg * 4 + j) for j in range(4)], keys, "xg%d" % fg)
                loaded[i] = (fg, keys)

            issue(0)
            issue(1)
            for i, (bi, o, n, kg) in enumerate(groups):
                if i + 2 < len(groups):
                    issue(i + 2)
                if kg == 0 and l + 1 < depth:
                    for cg in range(4 * bi, 4 * bi + 4):
                        ada_group(l + 1, cg)
                fg, keys = loaded.pop(i)
                hs = i % 2
                hview = Hp[:, 0:4, :] if hs == 0 else hst[:, :, :]
                hkeys = [("H", j) for j in range(4)] if hs == 0 else [("hst", j) for j in range(4)]
                col = 1 if o == 0 else 0
                for j in range(4):
                    k = kg * 4 + j
                    xc = Fp[:, fg * 4 + j, :]
                    tt(xc[:, 0:n], xc[:, 0:n], rstd_b[:, o:o + n], ALU.mult, [keys[j]] + RSB, [keys[j]])
                    act(hview[:, j, 0:n], xc[:, 0:n], AF.Identity, [keys[j], ("gs", l), ("mod", l)], [hkeys[j]],
                        scale=gs[:, l, k, col:col + 1], bias=mod[:, l, k, col:col + 1])
                dma("sp", hx_in[bi].ap().rearrange("(k p) t -> p k t", p=128)[:, kg * 4:(kg + 1) * 4, :], hview[:, :, 0:n],
                    hkeys, [("hx_in", bi)], "hxst%d" % bi)
                if kg == 3:
                    P.add("pool", lambda e, bi=bi: e.collective_compute("AllGather", ALU.bypass,
                                                                        replica_groups=[[0, 1], [2, 3], [4, 5], [6, 7]],
                                                                        ins=[hx_in[bi].ap().opt()], outs=[hx_all[bi].ap().opt()]),
                          [("hx_in", bi)], [("hx_all", bi)], chan="cc", inc=1)
            if l + 1 < depth:
                ada_gs(l + 1)

        def load_hblock(r, o, n):
            i = st["s"] % 2
            st["s"] += 1
            key = P.akey("A", ("hb", i))
            bi = [b[0] for b in OWN_BLOCKS].index(o)
            src = hx_all[bi].ap().rearrange("(r k p) t -> r p k t", r=2, p=128)[r]
            dma("sp", hb_v[i][:, :, 0:n], src[:, :, 0:n], [("hx_all", bi)], [key], "hb%d" % i)
            return hb_v[i], key

        def load_tab(src, pair, o, n):
            tb, tk, tchan = tab_slot()
            dma("sp", tb[:, :, 0:n], src[:, 2 * pair:2 * pair + 2, o:o + n], [], [tk], tchan)
            return tb, tk

        def phase2(l):
            set_gpool([0, 1, 2, 3, 4, 5])
            KGK = P.akey("C", "Kg")
            VGK = P.akey("C", "Vg")
            slot, wkey, wchan = wa_slot()
            dma("pool", slot[:, :, :], w_in[l][:, C_KV:C_KV + 512].rearrange("(k p) c -> p k c", p=128), [], [wkey], wchan)
            for (r, o, n) in KEY_BLOCKS:
                k0 = r * TO + o
                hb, hk = load_hblock(r, o, n)
                tb, tk = load_tab(tabk, 0, k0, n)
                for c in range(2):
                    pb, pk = Gb()
                    for kc in range(KC):
                        mm(pb[:, 0:n], slot[:, kc, c * 128:(c + 1) * 128], hb[:, kc, 0:n], kc == 0, kc == KC - 1, [wkey, hk], [pk])
                    normrope(pb[:, 0:n], pk, n, 128, tb, tk, KgT[:, c, k0:k0 + n], [KGK],
                             gain=lc(l, 66), gain_sw=lc(l, 67), norm=True)
                    tick()
                for tti in range(n // 128):
                    kt = (k0 // 128) + tti
                    pb, pk = Gb()
                    for kc in range(KC):
                        mm(pb[:, 0:256], hb[:, kc, tti * 128:(tti + 1) * 128], slot[:, kc, 256:512], kc == 0, kc == KC - 1, [wkey, hk], [pk])
                    act(Vg[:, kt, :], pb[:, 0:256], AF.Copy, [pk], [VGK])
                    tick()
            slot, wkey, wchan = wa_slot()
            dma("pool", slot[:, :, :], w_in[l][:, C_CKV:C_CKV + 512].rearrange("(k p) c -> p k c", p=128), [], [wkey], wchan)
            for (r, o, n) in KEY_BLOCKS:
                k0 = r * TO + o
                hb, hk = load_hblock(r, o, n)
                norm4(lambda kc: hb[:, kc, 0:n], [hk], n, slot, wkey, 72, l,
                      lambda c: ckvn[:, c, k0:k0 + n], [("ckvn",)])
            slot, wkey, wchan = wa_slot()
            dma("pool", slot[:, :, 0:64], w_in[l][:, C_KR:C_KR + 64].rearrange("(k p) c -> p k c", p=128), [], [wkey], wchan)
            for (r, o, n) in KEY_BLOCKS:
                k0 = r * TO + o
                hb, hk = load_hblock(r, o, n)
                tb, tk = load_tab(tabk, 1, k0, n)
                pb, pk = Gb()
                for kc in range(KC):
                    mm(pb[0:64, 0:n], slot[:, kc, 0:64], hb[:, kc, 0:n], kc == 0, kc == KC - 1, [wkey, hk], [pk])
                normrope(pb[0:64, 0:n], pk, n, 64, tb, tk, krT[0:64, k0:k0 + n], [("krT",)])
                tick()
            flush()

        def gate_proj(l, slot, wkey, coff, HTK, sgi):
            for (o, n) in OWN_BLOCKS:
                pb, pk = Gb()
                for kc in range(KC):
                    mm(pb[:, 0:n], slot[:, kc, coff:coff + 128], hTo[:, kc, o:o + n], kc == 0, kc == KC - 1, [wkey, HTK], [pk])
                e1, ek = Ft()
                act(e1[:, 0:n], pb[:, 0:n], AF.Exp, [pk], [ek], scale=-1.0)
                ts(e1[:, 0:n], e1[:, 0:n], 1.0, None, ALU.add, None, [ek], [ek])
                recip(e1[:, 0:n], e1[:, 0:n], [ek], [ek])
                tt(sg[sgi][:, o:o + n], pb[:, 0:n], e1[:, 0:n], ALU.mult, [pk, ek], [("sg", sgi, o)])
                tick()

        S_BANKS = [2, 3, 6, 7]
        LOOK = 3

        def attention(l, sgi, q1, q1key, k1_fn, k1key, v_fn, vkey, scale, mla, qr=None, qrkey=None):
            items = []
            for (o, n, kts) in [(128, 512, list(range(18))), (640, 512, list(range(18))), (0, 128, [0, 9])]:
                for i, kt in enumerate(kts):
                    items.append((o, n, kt, i, i == len(kts) - 1))
            live = {}
            cur = {}
            for idx in range(len(items) + LOOK):
                if idx < len(items):
                    o, n, kt, ii, last = items[idx]
                    si = S_BANKS[st["s"] % 4]
                    st["s"] += 1
                    Sb, Sk = ps[si], ("ps", si)
                    mm(Sb[:, 0:n], k1_fn(kt), q1[:, o:o + n], True, not mla, [k1key, q1key], [Sk])
                    if mla:
                        mm(Sb[:, 0:n], krT[:, kt * 128:(kt + 1) * 128], qr[:, o:o + n], False, True, [("krT",), qrkey], [Sk])
                    pt, ptk = Ht()
                    act(pt[:, 0:n], Sb[:, 0:n], AF.Exp, [Sk], [ptk], scale=scale)
                    live[idx] = (pt, ptk)
                j = idx - LOOK
                if j < 0:
                    continue
                o, n, kt, ii, last = items[j]
                first = ii == 0
                pt, ptk = live.pop(j)
                if first:
                    oi = st["o"] % 2
                    st["o"] += 1
                    cur = dict(Ob=ps[4 + oi], Ok=("ps", 4 + oi), acc=accs[oi], acck=("acc", oi), accd=accd[oi], accdk=("accd", oi))
                Ob, Ok, acc, acck, acd, acdk = cur["Ob"], cur["Ok"], cur["acc"], cur["acck"], cur["accd"], cur["accdk"]
                mm(Ob[:, 0:n], v_fn(kt), pt[:, 0:n], first, last, [vkey, ptk], [Ok])
                if ii == 0:
                    pcopy(acc[:, 0:n], pt[:, 0:n], [ptk], [acck])
                elif ii == 1:
                    vcopy(acd[:, 0:n], pt[:, 0:n], [ptk], [acdk])
                elif ii % 2 == 0:
                    ptt(acc[:, 0:n], acc[:, 0:n], pt[:, 0:n], ALU.add, [acck, ptk], [acck])
                else:
                    tt(acd[:, 0:n], acd[:, 0:n], pt[:, 0:n], ALU.add, [acdk, ptk], [acdk])
                if last:
                    Lb, Lk = Gb()
                    mm(Lb[:, 0:n], onesf[:, :], acc[:, 0:n], True, False, [acck, ("onesf",)], [Lk])
                    mm(Lb[:, 0:n], onesf[:, :], acd[:, 0:n], False, True, [acdk, ("onesf",)], [Lk])
                    ri, rk = Ft()
                    recip(ri[:, 0:n], Lb[:, 0:n], [Lk], [rk])
                    tt(ri[:, 0:n], Ob[:, 0:n], ri[:, 0:n], ALU.mult, [Ok, rk], [rk])
                    tt(sg[sgi][:, o:o + n], ri[:, 0:n], sg[sgi][:, o:o + n], ALU.mult, [rk, ("sg", sgi, o)], [("sg", sgi, o)])

        def phase3(l):
            P.fence("A", scr[:, 0:1])
            HTK = P.akey("A", "hTo")
            set_gpool([0, 1, 2, 3, 4, 5])
            for bi, (o, n) in enumerate(OWN_BLOCKS):
                dma("sp", hTo[:, :, o:o + n], hx_in[bi].ap().rearrange("(k p) t -> p k t", p=128), [("hx_in", bi)], [HTK], "hTo")
            slot, wkey, wchan = wa_slot()
            dma("pool", slot[:, :, :], w_in[l][:, C_CQ:C_CQ + 512].rearrange("(k p) c -> p k c", p=128), [], [wkey], wchan)
            for (o, n) in OWN_BLOCKS:
                norm4(lambda kc: hTo[:, kc, o:o + n], [HTK], n, slot, wkey, 68, l,
                      lambda c: cqn[:, c, o:o + n], [("cqn",)])
            set_gpool([0, 1])
            KGK = P.akey("C", "Kg")
            VGK = P.akey("C", "Vg")

            def prep_gqa(h):
                b = h % 2
                slot, wkey, wchan = wa_slot()
                dma("pool", slot[:, :, 0:256], w_in[l][:, C_QG + h * 256:C_QG + (h + 1) * 256].rearrange("(k p) c -> p k c", p=128),
                    [], [wkey], wchan)
                for (o, n) in OWN_BLOCKS:
                    tb, tk = load_tab(tabq, 0, o, n)
                    pb, pk = Gb()
                    for kc in range(KC):
                        mm(pb[:, 0:n], slot[:, kc, 0:128], hTo[:, kc, o:o + n], kc == 0, kc == KC - 1, [wkey, HTK], [pk])
                    normrope(pb[:, 0:n], pk, n, 128, tb, tk, Qg[b][:, o:o + n], [("Qg", b)],
                             gain=lc(l, 64), gain_sw=lc(l, 65), norm=True)
                    tick()
                gate_proj(l, slot, wkey, 128, HTK, b)
                flush()

            def attn_gqa(h):
                b = h % 2
                kvh = h // 4
                attention(l, b, Qg[b], ("Qg", b), lambda kt: KgT[:, kvh, kt * 128:(kt + 1) * 128], KGK,
                          lambda kt: Vg[:, kt, kvh * 128:(kvh + 1) * 128], VGK, 128.0 ** -0.5, False)
                dma("sp", yT[h * 128:(h + 1) * 128, :], sg[b][:, :], [("sg", b, o) for (o, n) in OWN_BLOCKS], [("yT", h)], "yst")

            prep_gqa(0)
            for h in range(8):
                if h + 1 < 8:
                    prep_gqa(h + 1)
                attn_gqa(h)

            P.fence("C", scr[:, 1:2])

            def prep_mla(h):
                b = h % 2
                KNK = P.akey("C", ("Kn", b))
                VMK = P.akey("C", ("Vm", b))
                wb, wbk = WB[b], ("WB", b)
                dma("pool", wb[:, :, :], w_mla[l, h].rearrange("(k p) c -> p k c", p=128), [], [wbk], "WB%d" % b)
                slot, wkey, wchan = wa_slot()
                dma("pool", slot[:, :, 0:128], w_in[l][:, C_GM + h * 128:C_GM + (h + 1) * 128].rearrange("(k p) c -> p k c", p=128),
                    [], [wkey], wchan)
                for (k0, n) in [(0, 512), (512, 512), (1024, 512), (1536, 512), (2048, 256)]:
                    pb, pk = Gb()
                    for kc in range(4):
                        mm(pb[:, 0:n], wb[:, kc, 192:320], ckvn[:, kc, k0:k0 + n], kc == 0, kc == 3, [wbk, ("ckvn",)], [pk])
                    act(Kn[b][:, k0:k0 + n], pb[:, 0:n], AF.Copy, [pk], [KNK])
                    tick()
                for g0 in range(0, 18, 4):
                    ng = min(4, 18 - g0)
                    pb, pk = Gb()
                    for j in range(ng):
                        kt = g0 + j
                        for kc in range(4):
                            mm(pb[:, j * 128:(j + 1) * 128], ckvn[:, kc, kt * 128:(kt + 1) * 128], wb[:, kc, 320:448],
                               kc == 0, kc == 3, [wbk, ("ckvn",)], [pk])
                    act(Vm[b][:, g0:g0 + ng, :], pb[:, 0:ng * 128].rearrange("p (a c) -> p a c", a=ng), AF.Copy, [pk], [VMK])
                    tick()
                for (o, n) in OWN_BLOCKS:
                    pb, pk = Gb()
                    for kc in range(4):
                        mm(pb[:, 0:n], wb[:, kc, 0:128], cqn[:, kc, o:o + n], kc == 0, kc == 3, [wbk, ("cqn",)], [pk])
                    act(Qn[b][:, o:o + n], pb[:, 0:n], AF.Copy, [pk], [("Qn", b)])
                    tick()
                    tb, tk = load_tab(tabq, 1, o, n)
                    pb, pk = Gb()
                    for kc in range(4):
                        mm(pb[0:64, 0:n], wb[:, kc, 128:192], cqn[:, kc, o:o + n], kc == 0, kc == 3, [wbk, ("cqn",)], [pk])
                    normrope(pb[0:64, 0:n], pk, n, 64, tb, tk, Qr[b][0:64, o:o + n], [("Qr", b)])
                    tick()
                gate_proj(l, slot, wkey, 0, HTK, b)
                flush()

            def attn_mla(h):
                b = h % 2
                KNK = P.akey("C", ("Kn", b))
                VMK = P.akey("C", ("Vm", b))
                attention(l, b, Qn[b], ("Qn", b), lambda kt: Kn[b][:, kt * 128:(kt + 1) * 128], KNK,
                          lambda kt: Vm[b][:, kt, :], VMK, 192.0 ** -0.5, True, qr=Qr[b], qrkey=("Qr", b))
                dma("sp", yT[(8 + h) * 128:(9 + h) * 128, :], sg[b][:, :], [("sg", b, o) for (o, n) in OWN_BLOCKS],
                    [("yT", 8 + h)], "yst")

            prep_mla(0)
            for h in range(8):
                if h + 1 < 8:
                    prep_mla(h + 1)
                attn_mla(h)
            P.fence("C", scr[:, 2:3])

        def phase4(l):
            P.fence("A", scr[:, 3:4])
            YK = P.akey("A", "yT")
            set_gpool([0, 1, 2, 3, 4])
            accb = [(ps[5], ("ps", 5)), (ps[6], ("ps", 6)), (ps[7], ("ps", 7))]
            dma("sp", hTo[:, :, :], yT.ap().rearrange("(k p) t -> p k t", p=128), [("yT", k) for k in range(KC)], [YK], "hTo")
            items = [(dc, bi, o, n) for dc in range(KC) for bi, (o, n) in enumerate(OWN_BLOCKS)]
            loaded = {}

            def issue(i):
                dc, bi, o, n = items[i]
                xb, xk = Ft()
                dma("sp", xb[:, 0:n], xT[dc * 128:(dc + 1) * 128, o:o + n], [("xT", dc)], [xk], "xc%d" % (st["f"] % NF))
                loaded[i] = (xb, xk)

            for i in range(3):
                issue(i)
            slot = wkey = None
            for i, (dc, bi, o, n) in enumerate(items):
                g, c = divmod(dc, 4)
                if c == 0 and bi == 0:
                    slot, wkey, wchan = wa_slot()
                    dma("pool", slot[:, :, :], w_out[l][:, g * 512:(g + 1) * 512].rearrange("(k p) c -> p k c", p=128), [], [wkey], wchan)
                if i + 3 < len(items):
                    issue(i + 3)
                col = 1 if o == 0 else 0
                xb, xk = loaded.pop(i)
                pb, pk = Gb()
                for kc in range(KC):
                    mm(pb[:, 0:n], slot[:, kc, c * 128:(c + 1) * 128], hTo[:, kc, o:o + n], kc == 0, kc == KC - 1, [wkey, YK], [pk])
                stt(xb[:, 0:n], pb[:, 0:n], mod[:, l, 32 + dc, col:col + 1], xb[:, 0:n], ALU.mult, ALU.add,
                    [pk, xk, ("mod", l)], [xk])
                tick()
                dma("sp", xT[dc * 128:(dc + 1) * 128, o:o + n], xb[:, 0:n], [xk], [("xT", dc)], "xTst")
                sq, sk = Ht()
                tt(sq[:, 0:n], xb[:, 0:n], xb[:, 0:n], ALU.mult, [xk], [sk])
                defer(lambda sq=sq, sk=sk, bi=bi, n=n, dc=dc: mm(accb[bi][0][:, 0:n], ones[:, :], sq[:, 0:n], dc == 0, dc == KC - 1,
                                                                 [sk, ("ones",)], [accb[bi][1]]))
            flush()
            for bi, (o, n) in enumerate(OWN_BLOCKS):
                rstd_from(accb[bi][0][:, 0:n], accb[bi][1], n, D, rstd_b[:, o:o + n], ("rstd_b", bi))
            P.fence("A", scr[:, 4:5])

        for l in range(depth):
            if nph >= 1:
                phase1(l)
            if nph >= 2:
                phase2(l)
            if nph >= 3:
                phase3(l)
            if nph >= 4:
                phase4(l)

        set_gpool([0, 1, 2, 3, 4])
        if depth == 0 or nph < 4:
            rms_stats(depth)
        out_v = out
        for t in range(8):
            tok0 = 128 + t * 128
            xs_keys = [("F", 4 + i) for i in range(4)]
            xo_keys = [("F", 8 + i) for i in range(4)]
            dma("sp", Fp[:, 4:8, :].rearrange("p a (j t) -> p (a j) t", j=4), xT_v[:, :, tok0:tok0 + 128],
                [("xT", k) for k in range(KC)], xs_keys, "xs")
            for kg in range(4):
                for j in range(4):
                    k = kg * 4 + j
                    stt(Fp[:, 4 + kg, j * 128:(j + 1) * 128], Fp[:, 4 + kg, j * 128:(j + 1) * 128],
                        cst[:, CC_FG + k:CC_FG + k + 1], rstd_b[:, tok0:tok0 + 128], ALU.mult, ALU.mult,
                        [xs_keys[kg], CST] + RSB, [xs_keys[kg]])
                pb, pk = Gb()
                for j in range(4):
                    tr(pb[:, j * 128:(j + 1) * 128], Fp[:, 4 + kg, j * 128:(j + 1) * 128], ident, [xs_keys[kg], CST], [pk])
                act(Fp[:, 8 + kg, :], pb[:, :], AF.Copy, [pk], [xo_keys[kg]])
            dma("sp", out_v[t * 128:(t + 1) * 128, :].rearrange("t (a b) -> t a b", a=4), Fp[:, 8:12, :], xo_keys, [("out", t)], "outst")
        if dbg_out is not None:
            if dbg == "hx":
                dma("pool", dbg_out[:, 128:640], hx_all[1][D:2 * D, :], [("hx_all", 1)], [("dbg",)], "dbgst")
            elif dbg == "y":
                dma("pool", dbg_out[:, :], yT.ap(), [("yT", k) for k in range(KC)], [("dbg",)], "dbgst")
            else:
                dma("sp", dbg_out[:, :], xT.ap(), [("xT", k) for k in range(KC)], [("dbg",)], "dbgst")
        print('sbuf bytes remaining', nc.sbuf_bytes_remaining)
        P.finalize(nc, es)
        with nc.Block() as block:
            @block.tensor
            def _(e):
                P.emit("pe", e)

            @block.scalar
            def _(e):
                P.emit("act", e)

            @block.vector
            def _(e):
                P.emit("dve", e)

            @block.gpsimd
            def _(e):
                P.emit("pool", e)

            @block.sync
            def _(e):
                P.emit("sp", e)
                e.wait_ge(P.sems[("chan", "outst")], P.chan_hist["outst"][-1][1])
                if ("chan", "dbgst") in P.sems:
                    e.wait_ge(P.sems[("chan", "dbgst")], P.chan_hist["dbgst"][-1][1])
    return nc


def _rope_tables(rot_dim):
    rows = SEQ // GRID_W
    row = np.repeat(np.arange(rows, dtype=np.float32), GRID_W)
    col = np.tile(np.arange(GRID_W, dtype=np.float32), rows)
    n_freq = rot_dim // 4
    inv = (np.float32(10000.0) ** (-np.arange(n_freq, dtype=np.float32) / np.float32(n_freq))).astype(np.float32)
    ang = np.concatenate([row[:, None] * inv[None], col[:, None] * inv[None]], axis=-1).astype(np.float32)
    return np.cos(ang).astype(np.float32), np.sin(ang).astype(np.float32)


def _feature_major_tables(rot_dim):
    cos, sin = _rope_tables(rot_dim)
    half = rot_dim // 2
    C = np.concatenate([cos.T, cos.T], axis=0)
    S = np.concatenate([-sin.T, sin.T], axis=0)
    assert C.shape == (rot_dim, SEQ) and half * 2 == rot_dim
    return C.astype(np.float32), S.astype(np.float32)


def _own_table(C, S, half):
    F = C.shape[0]
    c = np.concatenate([np.ones((F, 128), np.float32), C[:, half * 1024:(half + 1) * 1024]], axis=1)
    s = np.concatenate([np.zeros((F, 128), np.float32), S[:, half * 1024:(half + 1) * 1024]], axis=1)
    return c, s


def _prepare(x, c, ctx, c_ctx, w_ada, b_ada, norm_g, w_in, q_gain, k_gain, cq_gain, ckv_gain, w_uq, w_ukv, w_out, final_g):
    f = np.float32
    x = np.asarray(x, f); c = np.asarray(c, f); ctx = np.asarray(ctx, f); c_ctx = np.asarray(c_ctx, f)
    w_ada = np.ascontiguousarray(np.asarray(w_ada, f)); b_ada = np.asarray(b_ada, f); norm_g = np.asarray(norm_g, f)
    w_in = np.asarray(w_in, f); q_gain = np.asarray(q_gain, f); k_gain = np.asarray(k_gain, f)
    cq_gain = np.asarray(cq_gain, f); ckv_gain = np.asarray(ckv_gain, f)
    w_uq = np.asarray(w_uq, f); w_ukv = np.asarray(w_ukv, f); w_out = np.ascontiguousarray(np.asarray(w_out, f))
    final_g = np.asarray(final_g, f)
    q0, k0, v0, g0, cq0, ckv0, kr0, gm0 = 0, 1024, 1280, 1536, 2560, 3072, 3584, 3648
    cols = []
    for h in range(8):
        cols += list(range(q0 + h * 128, q0 + (h + 1) * 128)) + list(range(g0 + h * 128, g0 + (h + 1) * 128))
    cols += list(range(cq0, cq0 + 512))
    cols += list(range(gm0, gm0 + 1024))
    cols += list(range(k0, k0 + 256)) + list(range(v0, v0 + 256)) + list(range(ckv0, ckv0 + 512)) + list(range(kr0, kr0 + 64))
    assert len(cols) == 4672
    w_in_p = np.ascontiguousarray(w_in[:, :, cols])
    w_mla_p = np.empty((DEPTH, 8, 512, 448), f)
    for h in range(8):
        w_mla_p[:, h, :, 0:192] = w_uq[:, :, h * 192:(h + 1) * 192]
        w_mla_p[:, h, :, 192:448] = w_ukv[:, :, h * 256:(h + 1) * 256]
    C128, S128 = _feature_major_tables(128)
    C64, S64 = _feature_major_tables(64)
    own = []
    for half in range(2):
        t = np.zeros((128, 4, TO), f)
        t[:, 0], t[:, 1] = _own_table(C128, S128, half)
        t[0:64, 2], t[0:64, 3] = _own_table(C64, S64, half)
        own.append(t)
    tabk = np.ascontiguousarray(np.concatenate(own, axis=2))
    sw = (np.arange(128) + 64) % 128
    in_maps = []
    for core in range(8):
        b, half = core // 2, core % 2
        x_own = np.ascontiguousarray(np.concatenate([ctx[b, half * 128:(half + 1) * 128], x[b, half * 1024:(half + 1) * 1024]], axis=0))
        cst = np.zeros((128, NCC), f)
        cst[:, CC_ID:CC_ID + 128] = np.eye(128, dtype=f)
        for m in range(128):
            cst[(m + 64) % 128, CC_SW128 + m] = 1.0
        for m in range(64):
            cst[(m + 32) % 64, CC_SW64 + m] = 1.0
        for l in range(DEPTH):
            base = CC_L + l * CC_LW
            cst[:, base:base + 16] = norm_g[l].reshape(16, 128).T
            cst[:, base + 16:base + 64] = b_ada[l].reshape(48, 128).T
            cst[:, base + 64] = q_gain[l]
            cst[:, base + 65] = q_gain[l][sw]
            cst[:, base + 66] = k_gain[l]
            cst[:, base + 67] = k_gain[l][sw]
            cst[:, base + 68:base + 72] = cq_gain[l].reshape(4, 128).T
            cst[:, base + 72:base + 76] = ckv_gain[l].reshape(4, 128).T
        cst[:, CC_FG:CC_FG + 16] = final_g.reshape(16, 128).T
        cst[:, CC_CB:CC_CB + 16] = c[b].reshape(16, 128).T
        cst[:, CC_CC:CC_CC + 16] = c_ctx.reshape(16, 128).T
        in_maps.append({"x_own": x_own, "consts": cst, "tabq": own[half], "tabk": tabk, "w_ada": w_ada,
                        "w_in_p": w_in_p, "w_mla_p": w_mla_p, "w_out": w_out})
    return in_maps


def kernel(x, c, ctx, c_ctx, w_ada, b_ada, norm_g, w_in, q_gain, k_gain, cq_gain, ckv_gain, w_uq, w_ukv, w_out, final_g):
    in_maps = _prepare(x, c, ctx, c_ctx, w_ada, b_ada, norm_g, w_in, q_gain, k_gain, cq_gain, ckv_gain, w_uq, w_ukv, w_out, final_g)
    nc = build_program(DEPTH)
    res = run_bass_kernel_spmd(nc, in_maps, core_ids=list(range(8)))
    outp = np.empty((4, SEQ, D), np.float32)
    for core in range(8):
        b, half = core // 2, core % 2
        outp[b, half * 1024:(half + 1) * 1024] = np.asarray(res.results[core]["out"], np.float32)
    return outp
```

```python
import bisect
from contextlib import ExitStack

import numpy as np
import concourse.bass as bass
import concourse.mybir as mybir
from concourse.bass_utils import run_bass_kernel_spmd

F32 = mybir.dt.float32
BF16 = mybir.dt.bfloat16
ALU = mybir.AluOpType
AF = mybir.ActivationFunctionType

D = 2048
KC = 16
DEPTH = 4
TO = 1152
TA = 2304
SEQ = 2048
CTX = 256
GRID_W = 64
EPS = 1e-6
OWN_BLOCKS = [(0, 128), (128, 512), (640, 512)]
KEY_BLOCKS = [(r, o, n) for r in range(2) for (o, n) in OWN_BLOCKS]
C_QG = 0
C_CQ = 2048
C_GM = 2560
C_KV = 3584
C_CKV = 4096
C_KR = 4608
CC_ID = 0
CC_SW128 = 128
CC_SW64 = 256
CC_L = 320
CC_LW = 76
CC_FG = CC_L + DEPTH * CC_LW
CC_CB = CC_FG + 16
CC_CC = CC_CB + 16
NCC = CC_CC + 16
EPOCH = 8000


class Prog:
    def __init__(self):
        self.ops = []
        self.last_w = {}
        self.readers = {}
        self.chan_hist = {}
        self.arena_gen = {}
        self.arena_keys = {}
        self.arena_fence = {}

    def akey(self, arena, sub):
        g = self.arena_gen.setdefault(arena, 0)
        k = ("A", arena, g, sub)
        ks = self.arena_keys.setdefault((arena, g), set())
        if k not in ks:
            ks.add(k)
            f = self.arena_fence.get((arena, g))
            if f is not None and k not in self.last_w:
                self.last_w[k] = f
                self.readers[k] = []
        return k

    def add(self, eng, fn, r=(), w=(), chan=None, inc=16):
        oid = len(self.ops)
        deps = set()
        for k in r:
            lw = self.last_w.get(k)
            if lw is not None:
                deps.add(lw)
        for k in w:
            lw = self.last_w.get(k)
            if lw is not None:
                deps.add(lw)
            deps.update(self.readers.get(k, ()))
        deps.discard(oid)
        for k in r:
            self.readers.setdefault(k, []).append(oid)
        for k in w:
            self.last_w[k] = oid
            self.readers[k] = []
        self.ops.append(dict(eng=eng, fn=fn, deps=deps, chan=chan, inc=inc, signal=False))
        return oid

    def fence(self, arena, tile):
        g = self.arena_gen.setdefault(arena, 0)
        keys = list(self.arena_keys.get((arena, g), ()))
        oid = self.add("dve", lambda e: e.memset(tile, 0.0), r=(), w=keys + [("fence", arena)])
        self.arena_gen[arena] = g + 1
        self.arena_fence[(arena, g + 1)] = oid
        return oid

    def finalize(self, nc, es):
        ops = self.ops
        for o in ops:
            for d in o["deps"]:
                if ops[d]["eng"] == "pe" and o["eng"] == "pe" and ops[d]["chan"] is None:
                    continue
                ops[d]["signal"] = True
        cnt = {}
        self.sems = {}
        for i, o in enumerate(ops):
            if o["chan"] is not None:
                c = o["chan"]
                cum = (self.chan_hist[c][-1][1] if c in self.chan_hist else 0) + o["inc"]
                self.chan_hist.setdefault(c, []).append((i, cum))
                o["sig"] = (("chan", c), cum)
                if ("chan", c) not in self.sems:
                    self.sems[("chan", c)] = es.enter_context(nc.semaphore("c_" + c))
            elif o["signal"]:
                n = cnt.get(o["eng"], 0)
                ep, v = divmod(n, EPOCH)
                cnt[o["eng"]] = n + 1
                sk = ("eng", o["eng"], ep)
                if sk not in self.sems:
                    self.sems[sk] = es.enter_context(nc.semaphore("e_%s_%d" % (o["eng"], ep)))
                o["sig"] = (sk, v + 1)
        for i, o in enumerate(ops):
            waits = {}
            for d in o["deps"]:
                od = ops[d]
                if od["eng"] == "pe" and o["eng"] == "pe" and od["chan"] is None:
                    continue
                if od["chan"] is not None:
                    hist = self.chan_hist[od["chan"]]
                    j = bisect.bisect_left(hist, (i, -1)) - 1
                    sk, val = ("chan", od["chan"]), hist[j][1]
                else:
                    sk, val = od["sig"]
                if waits.get(sk, 0) < val:
                    waits[sk] = val
            o["waits"] = waits

    def emit(self, engname, e):
        seen = {}
        for o in self.ops:
            if o["eng"] != engname:
                continue
            for sk, val in o["waits"].items():
                if seen.get(sk, 0) >= val:
                    continue
                seen[sk] = val
                e.wait_ge(self.sems[sk], val)
            ins = o["fn"](e)
            if o["chan"] is not None:
                ins.then_inc(self.sems[o["sig"][0]], o["inc"])
            elif o["signal"]:
                ins.then_inc(self.sems[o["sig"][0]], 1)


def build_program(depth=DEPTH, dbg=None, wd=DEPTH, nph=4):
    nc = bass.Bass("TRN2", target_bir_lowering=False)
    x_in = nc.dram_tensor("x_own", [TO, D], F32, kind="ExternalInput").ap()
    cst_in = nc.dram_tensor("consts", [128, NCC], F32, kind="ExternalInput").ap()
    tabq = nc.dram_tensor("tabq", [128, 4, TO], F32, kind="ExternalInput").ap()
    tabk = nc.dram_tensor("tabk", [128, 4, TA], F32, kind="ExternalInput").ap()
    w_ada = nc.dram_tensor("w_ada", [wd, D, 3 * D // 2], F32, kind="ExternalInput").ap()
    w_in = nc.dram_tensor("w_in_p", [wd, D, 4672], F32, kind="ExternalInput").ap()
    w_mla = nc.dram_tensor("w_mla_p", [wd, 8, 512, 448], F32, kind="ExternalInput").ap()
    w_out = nc.dram_tensor("w_out", [wd, D, D], F32, kind="ExternalInput").ap()
    out = nc.dram_tensor("out", [1024, D], F32, kind="ExternalOutput").ap()
    xT = nc.dram_tensor("xT_s", [D, TO], F32)
    hx_in = [nc.dram_tensor("hx_in%d" % i, [D, n], BF16) for i, (o, n) in enumerate(OWN_BLOCKS)]
    hx_all = [nc.dram_tensor("hx_all%d" % i, [2 * D, n], BF16) for i, (o, n) in enumerate(OWN_BLOCKS)]
    yT = nc.dram_tensor("yT_s", [D, TO], BF16)
    modx_in = nc.dram_tensor("modx_in", [128, 48], F32)
    modx_all = nc.dram_tensor("modx_all", [256, 48], F32)
    dbg_out = None
    if dbg is not None:
        dbg_out = nc.dram_tensor("dbg", [D, TO], F32, kind="ExternalOutput").ap()

    P = Prog()
    es = ExitStack()
    with es:
        def sb(name, shape, dt):
            return es.enter_context(nc.sbuf_tensor(name, shape, dt))

        arenaA = sb("arenaA", [128, KC * TO], BF16)
        arenaC = sb("arenaC", [128, 9216], BF16)
        ckvn = sb("ckvn", [128, 4, TA], BF16)
        krT = sb("krT", [128, TA], BF16)
        Qg = [sb("Qg%d" % i, [128, TO], BF16) for i in range(2)]
        Qn = [sb("Qn%d" % i, [128, TO], BF16) for i in range(2)]
        Qr = [sb("Qr%d" % i, [128, TO], BF16) for i in range(2)]
        sg = [sb("sg%d" % i, [128, TO], BF16) for i in range(2)]
        cqn = sb("cqn", [128, 4, TO], BF16)
        WA = [sb("WA%d" % i, [128, KC, 512], BF16) for i in range(2)]
        WB = [sb("WB%d" % i, [128, 4, 448], BF16) for i in range(2)]
        tabs = [sb("tab%d" % i, [128, 2, 512], F32) for i in range(2)]
        NF = 12
        Fp = sb("Fp", [128, NF, 512], F32)
        NH = 6
        Hp = sb("Hp", [128, NH, 512], BF16)
        hst = sb("hst", [128, 4, 512], BF16)
        rstd_b = sb("rstd_b", [128, TO], F32)
        accs = [sb("acc%d" % i, [128, 512], F32) for i in range(2)]
        accd = [sb("accd%d" % i, [128, 512], F32) for i in range(2)]
        cst = sb("cst", [128, NCC], F32)
        mod = sb("mod", [128, DEPTH, 48, 2], F32)
        gs = sb("gs", [128, DEPTH, 16, 2], F32)
        ones = sb("ones", [128, 128], BF16)
        onesf = sb("onesf", [128, 128], F32)
        scT = sb("scT", [128, 16, 2], BF16)
        mrow = sb("mrow", [2, 512], F32)
        modh = sb("modh", [128, 24, 2], F32)
        modg = sb("modg", [128, 48, 2], F32)
        scr = sb("scr", [128, 8], F32)
        ps = [es.enter_context(nc.psum_tensor("ps%d" % i, [128, 512], F32)) for i in range(8)]

        hb_v = [arenaA[:, i * 8192:(i + 1) * 8192].rearrange("p (k t) -> p k t", k=KC) for i in range(2)]
        hTo = arenaA[:, :].rearrange("p (k t) -> p k t", k=KC)
        KgT = arenaC[:, 0:4608].rearrange("p (c t) -> p c t", c=2)
        Vg = arenaC[:, 4608:9216].rearrange("p (t c) -> p t c", t=18)
        Kn = [arenaC[:, i * 2304:(i + 1) * 2304] for i in range(2)]
        Vm = [arenaC[:, 4608 + i * 2304:4608 + (i + 1) * 2304].rearrange("p (t c) -> p t c", t=18) for i in range(2)]

        ident = cst[:, CC_ID:CC_ID + 128]
        psw128 = cst[:, CC_SW128:CC_SW128 + 128]
        psw64 = cst[0:64, CC_SW64:CC_SW64 + 64]

        def lc(l, off, n=1):
            b = CC_L + l * CC_LW + off
            return cst[:, b:b + n]

        st = dict(f=0, h=0, g=0, tab=0, s=0, o=0, wa=0)

        def Ft():
            i = st["f"] % NF
            st["f"] += 1
            return Fp[:, i, :], ("F", i)

        def Ht():
            i = st["h"] % NH
            st["h"] += 1
            return Hp[:, i, :], ("H", i)

        gpool = [0, 1]

        def Gb():
            i = gpool[st["g"] % len(gpool)]
            st["g"] += 1
            return ps[i], ("ps", i)

        def set_gpool(lst):
            gpool[:] = lst
            st["g"] = 0

        def wa_slot():
            i = st["wa"] % 2
            st["wa"] += 1
            return WA[i], ("WA", i), "WA%d" % i

        def tab_slot():
            i = st["tab"] % 2
            st["tab"] += 1
            return tabs[i], ("tab", i), "tab%d" % i

        def mm(out_, lhsT, rhs, start, stop, r, w):
            P.add("pe", lambda e: e.matmul(out=out_, lhsT=lhsT, rhs=rhs, start=start, stop=stop), r, w)

        def tr(out_, in_, idn, r, w):
            P.add("pe", lambda e: e.transpose(out=out_, in_=in_, identity=idn), r, w)

        def act(out_, in_, func, r, w, scale=1.0, bias=0.0):
            P.add("act", lambda e: e.activation(out=out_, in_=in_, func=func, bias=bias, scale=scale), r, w)

        def tt(out_, in0, in1, op, r, w):
            P.add("dve", lambda e: e.tensor_tensor(out=out_, in0=in0, in1=in1, op=op), r, w)

        def ts(out_, in0, s1, s2, op0, op1, r, w):
            if op1 is None:
                P.add("dve", lambda e: e.tensor_scalar(out=out_, in0=in0, scalar1=s1, scalar2=0.0, op0=op0, op1=ALU.add), r, w)
            else:
                P.add("dve", lambda e: e.tensor_scalar(out=out_, in0=in0, scalar1=s1, scalar2=s2, op0=op0, op1=op1), r, w)

        def stt(out_, in0, scalar, in1, op0, op1, r, w):
            P.add("dve", lambda e: e.scalar_tensor_tensor(out=out_, in0=in0, scalar=scalar, in1=in1, op0=op0, op1=op1), r, w)

        def recip(out_, in_, r, w):
            P.add("dve", lambda e: e.reciprocal(out=out_, in_=in_), r, w)

        def vcopy(out_, in_, r, w):
            P.add("dve", lambda e: e.tensor_copy(out=out_, in_=in_), r, w)

        def dma(eng, out_, in_, r, w, chan):
            P.add(eng, lambda e: e.dma_start(out=out_, in_=in_), r, w, chan=chan)

        pend = []
        gc = [0]

        def defer(fn, lag=1):
            pend.append((gc[0] + lag, fn))

        def tick():
            gc[0] += 1
            run = [p for p in pend if p[0] <= gc[0]]
            pend[:] = [p for p in pend if p[0] > gc[0]]
            for _, fn in run:
                fn()

        def flush():
            while pend:
                tick()

        def ptt(out_, in0, in1, op, r, w):
            P.add("pool", lambda e: e.tensor_tensor(out=out_, in0=in0, in1=in1, op=op), r, w)

        def pcopy(out_, in_, r, w):
            P.add("pool", lambda e: e.tensor_copy(out=out_, in_=in_), r, w)

        def rstd_from(ss_ap, sskey, n, nfeat, out_ap, outkey):
            act(out_ap, ss_ap, AF.Sqrt, [sskey], [outkey], scale=1.0 / nfeat, bias=EPS)
            recip(out_ap, out_ap, [outkey], [outkey])

        CST = ("cst",)

        def normrope(psA, pskey, n, F, tab, tabkey, out_ap, outkeys, gain=None, gain_sw=None, norm=False):
            raw, rk = Ft()
            act(raw[0:F, 0:n], psA, AF.Copy, [pskey], [rk])
            sq, sk = (None, None)
            if norm:
                sq, sk = Ht()
                tt(sq[0:F, 0:n], raw[0:F, 0:n], raw[0:F, 0:n], ALU.mult, [rk], [sk])

            def tail():
                if norm:
                    ssb, ssk = Gb()
                    mm(ssb[0:F, 0:n], ones[0:F, 0:F], sq[0:F, 0:n], True, True, [sk, ("ones",)], [ssk])
                    rs, rsk = Ft()
                    rstd_from(ssb[0:F, 0:n], ssk, n, F, rs[0:F, 0:n], rsk)
                swb, swk = Gb()
                pw = psw128 if F == 128 else psw64
                mm(swb[0:F, 0:n], pw, raw[0:F, 0:n], True, True, [rk, CST], [swk])
                t1, t1k = Ft()
                t2, t2k = Ft()
                if gain is not None:
                    stt(t1[0:F, 0:n], raw[0:F, 0:n], gain, tab[0:F, 0, 0:n], ALU.mult, ALU.mult, [rk, tabkey, CST], [t1k])
                    stt(t2[0:F, 0:n], swb[0:F, 0:n], gain_sw, tab[0:F, 1, 0:n], ALU.mult, ALU.mult, [swk, tabkey, CST], [t2k])
                else:
                    tt(t1[0:F, 0:n], raw[0:F, 0:n], tab[0:F, 0, 0:n], ALU.mult, [rk, tabkey], [t1k])
                    tt(t2[0:F, 0:n], swb[0:F, 0:n], tab[0:F, 1, 0:n], ALU.mult, [swk, tabkey], [t2k])
                if norm:
                    tt(t1[0:F, 0:n], t1[0:F, 0:n], t2[0:F, 0:n], ALU.add, [t1k, t2k], [t1k])
                    tt(out_ap, t1[0:F, 0:n], rs[0:F, 0:n], ALU.mult, [t1k, rsk], outkeys)
                else:
                    tt(out_ap, t1[0:F, 0:n], t2[0:F, 0:n], ALU.add, [t1k, t2k], outkeys)
            defer(tail, 2)

        def norm4(src_fn, srckeys, n, wslot, wkey, gain_off, l, out_fn, outkeys):
            raws = []
            ssb, ssk = ps[6], ("ps", 6)
            for c in range(4):
                pb, pk = Gb()
                for kc in range(KC):
                    mm(pb[:, 0:n], wslot[:, kc, c * 128:(c + 1) * 128], src_fn(kc), kc == 0, kc == KC - 1,
                       [wkey] + srckeys, [pk])
                raw, rk = Ft()
                act(raw[:, 0:n], pb[:, 0:n], AF.Copy, [pk], [rk])
                sq, sk = Ht()
                tt(sq[:, 0:n], raw[:, 0:n], raw[:, 0:n], ALU.mult, [rk], [sk])
                defer(lambda sq=sq, sk=sk, c=c: mm(ssb[:, 0:n], ones[:, :], sq[:, 0:n], c == 0, c == 3, [sk, ("ones",)], [ssk]), 2)
                tick()
                raws.append((raw, rk))
            flush()
            rs, rsk = Ft()
            rstd_from(ssb[:, 0:n], ssk, n, 512, rs[:, 0:n], rsk)
            for c in range(4):
                raw, rk = raws[c]
                stt(out_fn(c), raw[:, 0:n], lc(l, gain_off + c), rs[:, 0:n], ALU.mult, ALU.mult, [rk, rsk, CST], outkeys)

        dma("sp", cst[:, :], cst_in[:, :], [], [CST], "cst")
        P.add("dve", lambda e: e.memset(ones[:, :], 1.0), [], [("ones",)])
        P.add("dve", lambda e: e.memset(onesf[:, :], 1.0), [], [("onesf",)])
        P.add("dve", lambda e: e.memset(krT[:, :], 0.0), [], [("krT",)])
        for i in range(2):
            P.add("dve", lambda e, i=i: e.memset(Qr[i][:, :], 0.0), [], [("Qr", i)])
        f0, f0k = Ft()
        act(f0[:, 0:32], cst[:, CC_CB:CC_CB + 32], AF.Exp, [CST], [f0k], scale=-1.0)
        ts(f0[:, 0:32], f0[:, 0:32], 1.0, None, ALU.add, None, [f0k], [f0k])
        recip(f0[:, 0:32], f0[:, 0:32], [f0k], [f0k])
        for j in range(2):
            tt(scT[:, :, j], f0[:, j * 16:(j + 1) * 16], cst[:, CC_CB + j * 16:CC_CB + (j + 1) * 16], ALU.mult,
               [f0k, CST], [("scT",)])

        def ada_group(l, cg):
            slot, wkey, wchan = wa_slot()
            dma("pool", slot[:, :, :], w_ada[l][:, cg * 512:(cg + 1) * 512].rearrange("(k p) c -> p k c", p=128),
                [], [wkey], wchan)
            pb, pk = Gb()
            for kc in range(KC):
                mm(pb[0:2, 0:512], scT[:, kc, :], slot[:, kc, :], kc == 0, kc == KC - 1, [wkey, ("scT",)], [pk])
            act(mrow[0:2, 0:512], pb[0:2, 0:512], AF.Copy, [pk], [("mrow",)])
            for j in range(4):
                ch = cg * 4 + j
                pb2, pk2 = Gb()
                tr(pb2[:, 0:2], mrow[0:2, j * 128:(j + 1) * 128], ident[0:2, 0:2], [("mrow",), CST], [pk2])
                vcopy(modh[:, ch, :], pb2[:, 0:2], [pk2], [("modh",)])

        def ada_gather(l):
            dma("pool", modx_in.ap().rearrange("p (c j) -> p c j", j=2), modh[:, :, :], [("modh",)], [("modx_in",)], "modst")
            P.add("pool", lambda e: e.collective_compute("AllGather", ALU.bypass,
                                                         replica_groups=[[0, 1], [2, 3], [4, 5], [6, 7]],
                                                         ins=[modx_in.ap().opt()], outs=[modx_all.ap().opt()]),
                  [("modx_in",)], [("modx_all",)], chan="cc", inc=1)
            dma("pool", modg[:, :, :].rearrange("p (r c) j -> p r c j", r=2),
                modx_all.ap().rearrange("(r p) (c j) -> p r c j", r=2, j=2), [("modx_all",)], [("modg",)], "modld")
            for ch in range(48):
                ts(mod[:, l, ch, :], modg[:, ch, :], lc(l, 16 + ch), None, ALU.add, None, [("modg",), CST], [("mod", l)])

        def ada_gs(l):
            for k in range(KC):
                ts(gs[:, l, k, :], mod[:, l, 16 + k, :], 1.0, lc(l, k), ALU.add, ALU.mult, [("mod", l), CST], [("gs", l)])

        set_gpool([0, 1, 2, 3, 4, 5])
        xT_v = xT.ap().rearrange("(k p) t -> p k t", p=128)
        for t in range(9):
            xs_keys = [("F", 4 + i) for i in range(4)]
            xo_keys = [("F", 8 + i) for i in range(4)]
            dma("sp", Fp[:, 4:8, :], x_in[t * 128:(t + 1) * 128, :].rearrange("t (a b) -> t a b", a=4), [], xs_keys, "xs")
            for kg in range(4):
                pb, pk = Gb()
                for j in range(4):
                    k = kg * 4 + j
                    tr(pb[:, j * 128:(j + 1) * 128], Fp[:, 4 + kg, j * 128:(j + 1) * 128], ident, xs_keys + [CST], [pk])
                act(Fp[:, 8 + kg, :], pb[:, :], AF.Copy, [pk], [xo_keys[kg]])
            dma("sp", xT_v[:, :, t * 128:(t + 1) * 128],
                Fp[:, 8:12, :].rearrange("p a (j t) -> p (a j) t", j=4), xo_keys, [("xT", k) for k in range(KC)], "xTst")
            if depth >= 1 and t < 6:
                ada_group(0, t)

        if depth >= 1:
            ada_gather(0)
            ada_gs(0)

        def rms_stats(l):
            accs = [(ps[5], ("ps", 5)), (ps[6], ("ps", 6)), (ps[7], ("ps", 7))]
            for k in range(KC):
                for bi, (o, n) in enumerate([(0, 384), (384, 384), (768, 384)]):
                    xc, xk = Ft()
                    dma("sp", xc[:, 0:n], xT[k * 128:(k + 1) * 128, o:o + n], [("xT", k)], [xk], "xc%d" % (st["f"] % NF))
                    sq, sk = Ht()
                    tt(sq[:, 0:n], xc[:, 0:n], xc[:, 0:n], ALU.mult, [xk], [sk])
                    mm(accs[bi][0][:, 0:n], ones[:, :], sq[:, 0:n], k == 0, k == KC - 1, [sk, ("ones",)], [accs[bi][1]])
            for bi, (o, n) in enumerate([(0, 384), (384, 384), (768, 384)]):
                rstd_from(accs[bi][0][:, 0:n], accs[bi][1], n, D, rstd_b[:, o:o + n], ("rstd_b", bi))

        RSB = [("rstd_b", i) for i in range(3)]

        def phase1(l):
            set_gpool([0, 1, 2, 3, 4])
            if l == 0 or nph < 4:
                rms_stats(l)
            xT_v3 = xT.ap().rearrange("(k p) t -> p k t", p=128)
            groups = [(bi, o, n, kg) for bi, (o, n) in enumerate(OWN_BLOCKS) for kg in range(4)]
            loaded = {}

            def issue(i):
                bi, o, n, kg = groups[i]
                fg = i % 3
                keys = [("F", fg * 4 + j) for j in range(4)]
                dma("sp", Fp[:, fg * 4:(fg + 1) * 4, 0:n], xT_v3[:, kg * 4:(kg + 1) * 4, o:o + n],
                    [("xT", kg * 4 + j) for j in range(4)], keys, "xg%d" % fg)
                loaded[i] = (fg, keys)

            issue(0)
            issue(1)
            for i, (bi, o, n, kg) in enumerate(groups):
                if i + 2 < len(groups):
                    issue(i + 2)
                if kg == 0 and l + 1 < depth:
                    for cg in range(2 * bi, 2 * bi + 2):
                        ada_group(l + 1, cg)
                    if bi == 2:
                        ada_gather(l + 1)
                fg, keys = loaded.pop(i)
                hs = i % 2
                hview = Hp[:, 0:4, :] if hs == 0 else hst[:, :, :]
                hkeys = [("H", j) for j in range(4)] if hs == 0 else [("hst", j) for j in range(4)]
                col = 1 if o == 0 else 0
                for j in range(4):
                    k = kg * 4 + j
                    xc = Fp[:, fg * 4 + j, :]
                    tt(xc[:, 0:n], xc[:, 0:n], rstd_b[:, o:o + n], ALU.mult, [keys[j]] + RSB, [keys[j]])
                    act(hview[:, j, 0:n], xc[:, 0:n], AF.Identity, [keys[j], ("gs", l), ("mod", l)], [hkeys[j]],
                        scale=gs[:, l, k, col:col + 1], bias=mod[:, l, k, col:col + 1])
                dma("sp", hx_in[bi].ap().rearrange("(k p) t -> p k t", p=128)[:, kg * 4:(kg + 1) * 4, :], hview[:, :, 0:n],
                    hkeys, [("hx_in", bi)], "hxst%d" % bi)
                if kg == 3:
                    P.add("pool", lambda e, bi=bi: e.collective_compute("AllGather", ALU.bypass,
                                                                        replica_groups=[[0, 1], [2, 3], [4, 5], [6, 7]],
                                                                        ins=[hx_in[bi].ap().opt()], outs=[hx_all[bi].ap().opt()]),
                          [("hx_in", bi)], [("hx_all", bi)], chan="cc", inc=1)
            if l + 1 < depth:
                ada_gs(l + 1)

        def load_hblock(r, o, n):
            i = st["s"] % 2
            st["s"] += 1
            key = P.akey("A", ("hb", i))
            bi = [b[0] for b in OWN_BLOCKS].index(o)
            src = hx_all[bi].ap().rearrange("(r k p) t -> r p k t", r=2, p=128)[r]
            dma("sp", hb_v[i][:, :, 0:n], src[:, :, 0:n], [("hx_all", bi)], [key], "hb%d" % i)
            return hb_v[i], key

        def load_tab(src, pair, o, n):
            tb, tk, tchan = tab_slot()
            dma("sp", tb[:, :, 0:n], src[:, 2 * pair:2 * pair + 2, o:o + n], [], [tk], tchan)
            return tb, tk

        def phase2(l):
            set_gpool([0, 1, 2, 3, 4, 5])
            KGK = P.akey("C", "Kg")
            VGK = P.akey("C", "Vg")
            slot, wkey, wchan = wa_slot()
            dma("pool", slot[:, :, :], w_in[l][:, C_KV:C_KV + 512].rearrange("(k p) c -> p k c", p=128), [], [wkey], wchan)
            for (r, o, n) in KEY_BLOCKS:
                k0 = r * TO + o
                hb, hk = load_hblock(r, o, n)
                tb, tk = load_tab(tabk, 0, k0, n)
                for c in range(2):
                    pb, pk = Gb()
                    for kc in range(KC):
                        mm(pb[:, 0:n], slot[:, kc, c * 128:(c + 1) * 128], hb[:, kc, 0:n], kc == 0, kc == KC - 1, [wkey, hk], [pk])
                    normrope(pb[:, 0:n], pk, n, 128, tb, tk, KgT[:, c, k0:k0 + n], [KGK],
                             gain=lc(l, 66), gain_sw=lc(l, 67), norm=True)
                    tick()
                for tti in range(n // 128):
                    kt = (k0 // 128) + tti
                    pb, pk = Gb()
                    for kc in range(KC):
                        mm(pb[:, 0:256], hb[:, kc, tti * 128:(tti + 1) * 128], slot[:, kc, 256:512], kc == 0, kc == KC - 1, [wkey, hk], [pk])
                    act(Vg[:, kt, :], pb[:, 0:256], AF.Copy, [pk], [VGK])
                    tick()
            slot, wkey, wchan = wa_slot()
            dma("pool", slot[:, :, :], w_in[l][:, C_CKV:C_CKV + 512].rearrange("(k p) c -> p k c", p=128), [], [wkey], wchan)
            for (r, o, n) in KEY_BLOCKS:
                k0 = r * TO + o
                hb, hk = load_hblock(r, o, n)
                norm4(lambda kc: hb[:, kc, 0:n], [hk], n, slot, wkey, 72, l,
                      lambda c: ckvn[:, c, k0:k0 + n], [("ckvn",)])
            slot, wkey, wchan = wa_slot()
            dma("pool", slot[:, :, 0:64], w_in[l][:, C_KR:C_KR + 64].rearrange("(k p) c -> p k c", p=128), [], [wkey], wchan)
            for (r, o, n) in KEY_BLOCKS:
                k0 = r * TO + o
                hb, hk = load_hblock(r, o, n)
                tb, tk = load_tab(tabk, 1, k0, n)
                pb, pk = Gb()
                for kc in range(KC):
                    mm(pb[0:64, 0:n], slot[:, kc, 0:64], hb[:, kc, 0:n], kc == 0, kc == KC - 1, [wkey, hk], [pk])
                normrope(pb[0:64, 0:n], pk, n, 64, tb, tk, krT[0:64, k0:k0 + n], [("krT",)])
                tick()
            flush()

        def gate_proj(l, slot, wkey, coff, HTK, sgi):
            for (o, n) in OWN_BLOCKS:
                pb, pk = Gb()
                for kc in range(KC):
                    mm(pb[:, 0:n], slot[:, kc, coff:coff + 128], hTo[:, kc, o:o + n], kc == 0, kc == KC - 1, [wkey, HTK], [pk])
                e1, ek = Ft()
                act(e1[:, 0:n], pb[:, 0:n], AF.Exp, [pk], [ek], scale=-1.0)
                ts(e1[:, 0:n], e1[:, 0:n], 1.0, None, ALU.add, None, [ek], [ek])
                recip(e1[:, 0:n], e1[:, 0:n], [ek], [ek])
                tt(sg[sgi][:, o:o + n], pb[:, 0:n], e1[:, 0:n], ALU.mult, [pk, ek], [("sg", sgi, o)])
                tick()

        S_BANKS = [2, 3, 6, 7]
        LOOK = 3

        def attention(l, sgi, q1, q1key, k1_fn, k1key, v_fn, vkey, scale, mla, qr=None, qrkey=None):
            items = []
            for (o, n, kts) in [(128, 512, list(range(18))), (640, 512, list(range(18))), (0, 128, [0, 9])]:
                for i, kt in enumerate(kts):
                    items.append((o, n, kt, i, i == len(kts) - 1))
            live = {}
            cur = {}
            for idx in range(len(items) + LOOK):
                if idx < len(items):
                    o, n, kt, ii, last = items[idx]
                    si = S_BANKS[st["s"] % 4]
                    st["s"] += 1
                    Sb, Sk = ps[si], ("ps", si)
                    mm(Sb[:, 0:n], k1_fn(kt), q1[:, o:o + n], True, not mla, [k1key, q1key], [Sk])
                    if mla:
                        mm(Sb[:, 0:n], krT[:, kt * 128:(kt + 1) * 128], qr[:, o:o + n], False, True, [("krT",), qrkey], [Sk])
                    pt, ptk = Ht()
                    act(pt[:, 0:n], Sb[:, 0:n], AF.Exp, [Sk], [ptk], scale=scale)
                    live[idx] = (pt, ptk)
                j = idx - LOOK
                if j < 0:
                    continue
                o, n, kt, ii, last = items[j]
                first = ii == 0
                pt, ptk = live.pop(j)
                if first:
                    oi = st["o"] % 2
                    st["o"] += 1
                    cur = dict(Ob=ps[4 + oi], Ok=("ps", 4 + oi), acc=accs[oi], acck=("acc", oi), accd=accd[oi], accdk=("accd", oi))
                Ob, Ok, acc, acck, acd, acdk = cur["Ob"], cur["Ok"], cur["acc"], cur["acck"], cur["accd"], cur["accdk"]
                mm(Ob[:, 0:n], v_fn(kt), pt[:, 0:n], first, last, [vkey, ptk], [Ok])
                if ii == 0:
                    pcopy(acc[:, 0:n], pt[:, 0:n], [ptk], [acck])
                elif ii == 1:
                    vcopy(acd[:, 0:n], pt[:, 0:n], [ptk], [acdk])
                elif ii % 2 == 0:
                    ptt(acc[:, 0:n], acc[:, 0:n], pt[:, 0:n], ALU.add, [acck, ptk], [acck])
                else:
                    tt(acd[:, 0:n], acd[:, 0:n], pt[:, 0:n], ALU.add, [acdk, ptk], [acdk])
                if last:
                    Lb, Lk = Gb()
                    mm(Lb[:, 0:n], onesf[:, :], acc[:, 0:n], True, False, [acck, ("onesf",)], [Lk])
                    mm(Lb[:, 0:n], onesf[:, :], acd[:, 0:n], False, True, [acdk, ("onesf",)], [Lk])
                    ri, rk = Ft()
                    recip(ri[:, 0:n], Lb[:, 0:n], [Lk], [rk])
                    tt(ri[:, 0:n], Ob[:, 0:n], ri[:, 0:n], ALU.mult, [Ok, rk], [rk])
                    tt(sg[sgi][:, o:o + n], ri[:, 0:n], sg[sgi][:, o:o + n], ALU.mult, [rk, ("sg", sgi, o)], [("sg", sgi, o)])

        def phase3(l):
            P.fence("A", scr[:, 0:1])
            HTK = P.akey("A", "hTo")
            set_gpool([0, 1, 2, 3, 4, 5])
            for bi, (o, n) in enumerate(OWN_BLOCKS):
                dma("sp", hTo[:, :, o:o + n], hx_in[bi].ap().rearrange("(k p) t -> p k t", p=128), [("hx_in", bi)], [HTK], "hTo")
            slot, wkey, wchan = wa_slot()
            dma("pool", slot[:, :, :], w_in[l][:, C_CQ:C_CQ + 512].rearrange("(k p) c -> p k c", p=128), [], [wkey], wchan)
            for (o, n) in OWN_BLOCKS:
                norm4(lambda kc: hTo[:, kc, o:o + n], [HTK], n, slot, wkey, 68, l,
                      lambda c: cqn[:, c, o:o + n], [("cqn",)])
            set_gpool([0, 1])
            KGK = P.akey("C", "Kg")
            VGK = P.akey("C", "Vg")

            def prep_gqa(h):
                b = h % 2
                slot, wkey, wchan = wa_slot()
                dma("pool", slot[:, :, 0:256], w_in[l][:, C_QG + h * 256:C_QG + (h + 1) * 256].rearrange("(k p) c -> p k c", p=128),
                    [], [wkey], wchan)
                for (o, n) in OWN_BLOCKS:
                    tb, tk = load_tab(tabq, 0, o, n)
                    pb, pk = Gb()
                    for kc in range(KC):
                        mm(pb[:, 0:n], slot[:, kc, 0:128], hTo[:, kc, o:o + n], kc == 0, kc == KC - 1, [wkey, HTK], [pk])
                    normrope(pb[:, 0:n], pk, n, 128, tb, tk, Qg[b][:, o:o + n], [("Qg", b)],
                             gain=lc(l, 64), gain_sw=lc(l, 65), norm=True)
                    tick()
                gate_proj(l, slot, wkey, 128, HTK, b)
                flush()

            def attn_gqa(h):
                b = h % 2
                kvh = h // 4
                attention(l, b, Qg[b], ("Qg", b), lambda kt: KgT[:, kvh, kt * 128:(kt + 1) * 128], KGK,
                          lambda kt: Vg[:, kt, kvh * 128:(kvh + 1) * 128], VGK, 128.0 ** -0.5, False)
                dma("sp", yT[h * 128:(h + 1) * 128, :], sg[b][:, :], [("sg", b, o) for (o, n) in OWN_BLOCKS], [("yT", h)], "yst")

            prep_gqa(0)
            for h in range(8):
                if h + 1 < 8:
                    prep_gqa(h + 1)
                attn_gqa(h)

            P.fence("C", scr[:, 1:2])

            def prep_mla(h):
                b = h % 2
                KNK = P.akey("C", ("Kn", b))
                VMK = P.akey("C", ("Vm", b))
                wb, wbk = WB[b], ("WB", b)
                dma("pool", wb[:, :, :], w_mla[l, h].rearrange("(k p) c -> p k c", p=128), [], [wbk], "WB%d" % b)
                slot, wkey, wchan = wa_slot()
                dma("pool", slot[:, :, 0:128], w_in[l][:, C_GM + h * 128:C_GM + (h + 1) * 128].rearrange("(k p) c -> p k c", p=128),
                    [], [wkey], wchan)
                for (k0, n) in [(0, 512), (512, 512), (1024, 512), (1536, 512), (2048, 256)]:
                    pb, pk = Gb()
                    for kc in range(4):
                        mm(pb[:, 0:n], wb[:, kc, 192:320], ckvn[:, kc, k0:k0 + n], kc == 0, kc == 3, [wbk, ("ckvn",)], [pk])
                    act(Kn[b][:, k0:k0 + n], pb[:, 0:n], AF.Copy, [pk], [KNK])
                    tick()
                for g0 in range(0, 18, 4):
                    ng = min(4, 18 - g0)
                    pb, pk = Gb()
                    for j in range(ng):
                        kt = g0 + j
                        for kc in range(4):
                            mm(pb[:, j * 128:(j + 1) * 128], ckvn[:, kc, kt * 128:(kt + 1) * 128], wb[:, kc, 320:448],
                               kc == 0, kc == 3, [wbk, ("ckvn",)], [pk])
                    act(Vm[b][:, g0:g0 + ng, :], pb[:, 0:ng * 128].rearrange("p (a c) -> p a c", a=ng), AF.Copy, [pk], [VMK])
                    tick()
                for (o, n) in OWN_BLOCKS:
                    pb, pk = Gb()
                    for kc in range(4):
                        mm(pb[:, 0:n], wb[:, kc, 0:128], cqn[:, kc, o:o + n], kc == 0, kc == 3, [wbk, ("cqn",)], [pk])
                    act(Qn[b][:, o:o + n], pb[:, 0:n], AF.Copy, [pk], [("Qn", b)])
                    tick()
                    tb, tk = load_tab(tabq, 1, o, n)
                    pb, pk = Gb()
                    for kc in range(4):
                        mm(pb[0:64, 0:n], wb[:, kc, 128:192], cqn[:, kc, o:o + n], kc == 0, kc == 3, [wbk, ("cqn",)], [pk])
                    normrope(pb[0:64, 0:n], pk, n, 64, tb, tk, Qr[b][0:64, o:o + n], [("Qr", b)])
                    tick()
                gate_proj(l, slot, wkey, 0, HTK, b)
                flush()

            def attn_mla(h):
                b = h % 2
                KNK = P.akey("C", ("Kn", b))
                VMK = P.akey("C", ("Vm", b))
                attention(l, b, Qn[b], ("Qn", b), lambda kt: Kn[b][:, kt * 128:(kt + 1) * 128], KNK,
                          lambda kt: Vm[b][:, kt, :], VMK, 192.0 ** -0.5, True, qr=Qr[b], qrkey=("Qr", b))
                dma("sp", yT[(8 + h) * 128:(9 + h) * 128, :], sg[b][:, :], [("sg", b, o) for (o, n) in OWN_BLOCKS],
                    [("yT", 8 + h)], "yst")

            prep_mla(0)
            for h in range(8):
                if h + 1 < 8:
                    prep_mla(h + 1)
                attn_mla(h)
            P.fence("C", scr[:, 2:3])

        def phase4(l):
            P.fence("A", scr[:, 3:4])
            YK = P.akey("A", "yT")
            set_gpool([0, 1, 2, 3, 4])
            accb = [(ps[5], ("ps", 5)), (ps[6], ("ps", 6)), (ps[7], ("ps", 7))]
            dma("sp", hTo[:, :, :], yT.ap().rearrange("(k p) t -> p k t", p=128), [("yT", k) for k in range(KC)], [YK], "hTo")
            items = [(dc, bi, o, n) for dc in range(KC) for bi, (o, n) in enumerate(OWN_BLOCKS)]
            loaded = {}

            def issue(i):
                dc, bi, o, n = items[i]
                xb, xk = Ft()
                dma("sp", xb[:, 0:n], xT[dc * 128:(dc + 1) * 128, o:o + n], [("xT", dc)], [xk], "xc%d" % (st["f"] % NF))
                loaded[i] = (xb, xk)

            for i in range(3):
                issue(i)
            slot = wkey = None
            for i, (dc, bi, o, n) in enumerate(items):
                g, c = divmod(dc, 4)
                if c == 0 and bi == 0:
                    slot, wkey, wchan = wa_slot()
                    dma("pool", slot[:, :, :], w_out[l][:, g * 512:(g + 1) * 512].rearrange("(k p) c -> p k c", p=128), [], [wkey], wchan)
                if i + 3 < len(items):
                    issue(i + 3)
                col = 1 if o == 0 else 0
                xb, xk = loaded.pop(i)
                pb, pk = Gb()
                for kc in range(KC):
                    mm(pb[:, 0:n], slot[:, kc, c * 128:(c + 1) * 128], hTo[:, kc, o:o + n], kc == 0, kc == KC - 1, [wkey, YK], [pk])
                stt(xb[:, 0:n], pb[:, 0:n], mod[:, l, 32 + dc, col:col + 1], xb[:, 0:n], ALU.mult, ALU.add,
                    [pk, xk, ("mod", l)], [xk])
                tick()
                dma("sp", xT[dc * 128:(dc + 1) * 128, o:o + n], xb[:, 0:n], [xk], [("xT", dc)], "xTst")
                sq, sk = Ht()
                tt(sq[:, 0:n], xb[:, 0:n], xb[:, 0:n], ALU.mult, [xk], [sk])
                defer(lambda sq=sq, sk=sk, bi=bi, n=n, dc=dc: mm(accb[bi][0][:, 0:n], ones[:, :], sq[:, 0:n], dc == 0, dc == KC - 1,
                                                                 [sk, ("ones",)], [accb[bi][1]]))
            flush()
            for bi, (o, n) in enumerate(OWN_BLOCKS):
                rstd_from(accb[bi][0][:, 0:n], accb[bi][1], n, D, rstd_b[:, o:o + n], ("rstd_b", bi))
            P.fence("A", scr[:, 4:5])

        for l in range(depth):
            if nph >= 1:
                phase1(l)
            if nph >= 2:
                phase2(l)
            if nph >= 3:
                phase3(l)
            if nph >= 4:
                phase4(l)

        set_gpool([0, 1, 2, 3, 4])
        if depth == 0 or nph < 4:
            rms_stats(depth)
        out_v = out
        for t in range(8):
            tok0 = 128 + t * 128
            xs_keys = [("F", 4 + i) for i in range(4)]
            xo_keys = [("F", 8 + i) for i in range(4)]
            dma("sp", Fp[:, 4:8, :].rearrange("p a (j t) -> p (a j) t", j=4), xT_v[:, :, tok0:tok0 + 128],
                [("xT", k) for k in range(KC)], xs_keys, "xs")
            for kg in range(4):
                for j in range(4):
                    k = kg * 4 + j
                    stt(Fp[:, 4 + kg, j * 128:(j + 1) * 128], Fp[:, 4 + kg, j * 128:(j + 1) * 128],
                        cst[:, CC_FG + k:CC_FG + k + 1], rstd_b[:, tok0:tok0 + 128], ALU.mult, ALU.mult,
                        [xs_keys[kg], CST] + RSB, [xs_keys[kg]])
                pb, pk = Gb()
                for j in range(4):
                    tr(pb[:, j * 128:(j + 1) * 128], Fp[:, 4 + kg, j * 128:(j + 1) * 128], ident, [xs_keys[kg], CST], [pk])
                act(Fp[:, 8 + kg, :], pb[:, :], AF.Copy, [pk], [xo_keys[kg]])
            dma("sp", out_v[t * 128:(t + 1) * 128, :].rearrange("t (a b) -> t a b", a=4), Fp[:, 8:12, :], xo_keys, [("out", t)], "outst")
        if dbg_out is not None:
            if dbg == "hx":
                dma("pool", dbg_out[:, 128:640], hx_all[1][D:2 * D, :], [("hx_all", 1)], [("dbg",)], "dbgst")
            elif dbg == "y":
                dma("pool", dbg_out[:, :], yT.ap(), [("yT", k) for k in range(KC)], [("dbg",)], "dbgst")
            else:
                dma("sp", dbg_out[:, :], xT.ap(), [("xT", k) for k in range(KC)], [("dbg",)], "dbgst")
        print('sbuf bytes remaining', nc.sbuf_bytes_remaining)
        P.finalize(nc, es)
        with nc.Block() as block:
            @block.tensor
            def _(e):
                P.emit("pe", e)

            @block.scalar
            def _(e):
                P.emit("act", e)

            @block.vector
            def _(e):
                P.emit("dve", e)

            @block.gpsimd
            def _(e):
                P.emit("pool", e)

            @block.sync
            def _(e):
                P.emit("sp", e)
                e.wait_ge(P.sems[("chan", "outst")], P.chan_hist["outst"][-1][1])
                if ("chan", "dbgst") in P.sems:
                    e.wait_ge(P.sems[("chan", "dbgst")], P.chan_hist["dbgst"][-1][1])
    return nc


def _rope_tables(rot_dim):
    rows = SEQ // GRID_W
    row = np.repeat(np.arange(rows, dtype=np.float32), GRID_W)
    col = np.tile(np.arange(GRID_W, dtype=np.float32), rows)
    n_freq = rot_dim // 4
    inv = (np.float32(10000.0) ** (-np.arange(n_freq, dtype=np.float32) / np.float32(n_freq))).astype(np.float32)
    ang = np.concatenate([row[:, None] * inv[None], col[:, None] * inv[None]], axis=-1).astype(np.float32)
    return np.cos(ang).astype(np.float32), np.sin(ang).astype(np.float32)


def _feature_major_tables(rot_dim):
    cos, sin = _rope_tables(rot_dim)
    half = rot_dim // 2
    C = np.concatenate([cos.T, cos.T], axis=0)
    S = np.concatenate([-sin.T, sin.T], axis=0)
    assert C.shape == (rot_dim, SEQ) and half * 2 == rot_dim
    return C.astype(np.float32), S.astype(np.float32)


def _own_table(C, S, half):
    F = C.shape[0]
    c = np.concatenate([np.ones((F, 128), np.float32), C[:, half * 1024:(half + 1) * 1024]], axis=1)
    s = np.concatenate([np.zeros((F, 128), np.float32), S[:, half * 1024:(half + 1) * 1024]], axis=1)
    return c, s


def _prepare(x, c, ctx, c_ctx, w_ada, b_ada, norm_g, w_in, q_gain, k_gain, cq_gain, ckv_gain, w_uq, w_ukv, w_out, final_g):
    f = np.float32
    x = np.asarray(x, f); c = np.asarray(c, f); ctx = np.asarray(ctx, f); c_ctx = np.asarray(c_ctx, f)
    w_ada = np.ascontiguousarray(np.asarray(w_ada, f)); b_ada = np.asarray(b_ada, f); norm_g = np.asarray(norm_g, f)
    w_in = np.asarray(w_in, f); q_gain = np.asarray(q_gain, f); k_gain = np.asarray(k_gain, f)
    cq_gain = np.asarray(cq_gain, f); ckv_gain = np.asarray(ckv_gain, f)
    w_uq = np.asarray(w_uq, f); w_ukv = np.asarray(w_ukv, f); w_out = np.ascontiguousarray(np.asarray(w_out, f))
    final_g = np.asarray(final_g, f)
    q0, k0, v0, g0, cq0, ckv0, kr0, gm0 = 0, 1024, 1280, 1536, 2560, 3072, 3584, 3648
    cols = []
    for h in range(8):
        cols += list(range(q0 + h * 128, q0 + (h + 1) * 128)) + list(range(g0 + h * 128, g0 + (h + 1) * 128))
    cols += list(range(cq0, cq0 + 512))
    cols += list(range(gm0, gm0 + 1024))
    cols += list(range(k0, k0 + 256)) + list(range(v0, v0 + 256)) + list(range(ckv0, ckv0 + 512)) + list(range(kr0, kr0 + 64))
    assert len(cols) == 4672
    w_in_p = np.ascontiguousarray(w_in[:, :, cols])
    w_mla_p = np.empty((DEPTH, 8, 512, 448), f)
    for h in range(8):
        w_mla_p[:, h, :, 0:192] = w_uq[:, :, h * 192:(h + 1) * 192]
        w_mla_p[:, h, :, 192:448] = w_ukv[:, :, h * 256:(h + 1) * 256]
    C128, S128 = _feature_major_tables(128)
    C64, S64 = _feature_major_tables(64)
    own = []
    for half in range(2):
        t = np.zeros((128, 4, TO), f)
        t[:, 0], t[:, 1] = _own_table(C128, S128, half)
        t[0:64, 2], t[0:64, 3] = _own_table(C64, S64, half)
        own.append(t)
    tabk = np.ascontiguousarray(np.concatenate(own, axis=2))
    sw = (np.arange(128) + 64) % 128
    w_ada_h = [np.ascontiguousarray(w_ada[:, :, r * 3072:(r + 1) * 3072]) for r in range(2)]
    in_maps = []
    for core in range(8):
        b, half = core // 2, core % 2
        x_own = np.ascontiguousarray(np.concatenate([ctx[b, half * 128:(half + 1) * 128], x[b, half * 1024:(half + 1) * 1024]], axis=0))
        cst = np.zeros((128, NCC), f)
        cst[:, CC_ID:CC_ID + 128] = np.eye(128, dtype=f)
        for m in range(128):
            cst[(m + 64) % 128, CC_SW128 + m] = 1.0
        for m in range(64):
            cst[(m + 32) % 64, CC_SW64 + m] = 1.0
        for l in range(DEPTH):
            base = CC_L + l * CC_LW
            cst[:, base:base + 16] = norm_g[l].reshape(16, 128).T
            cst[:, base + 16:base + 64] = b_ada[l].reshape(48, 128).T
            cst[:, base + 64] = q_gain[l]
            cst[:, base + 65] = q_gain[l][sw]
            cst[:, base + 66] = k_gain[l]
            cst[:, base + 67] = k_gain[l][sw]
            cst[:, base + 68:base + 72] = cq_gain[l].reshape(4, 128).T
            cst[:, base + 72:base + 76] = ckv_gain[l].reshape(4, 128).T
        cst[:, CC_FG:CC_FG + 16] = final_g.reshape(16, 128).T
        cst[:, CC_CB:CC_CB + 16] = c[b].reshape(16, 128).T
        cst[:, CC_CC:CC_CC + 16] = c_ctx.reshape(16, 128).T
        in_maps.append({"x_own": x_own, "consts": cst, "tabq": own[half], "tabk": tabk, "w_ada": w_ada_h[half],
                        "w_in_p": w_in_p, "w_mla_p": w_mla_p, "w_out": w_out})
    return in_maps


def kernel(x, c, ctx, c_ctx, w_ada, b_ada, norm_g, w_in, q_gain, k_gain, cq_gain, ckv_gain, w_uq, w_ukv, w_out, final_g):
    in_maps = _prepare(x, c, ctx, c_ctx, w_ada, b_ada, norm_g, w_in, q_gain, k_gain, cq_gain, ckv_gain, w_uq, w_ukv, w_out, final_g)
    nc = build_program(DEPTH)
    res = run_bass_kernel_spmd(nc, in_maps, core_ids=list(range(8)))
    outp = np.empty((4, SEQ, D), np.float32)
    for core in range(8):
        b, half = core // 2, core % 2
        outp[b, half * 1024:(half + 1) * 1024] = np.asarray(res.results[core]["out"], np.float32)
    return outp
```
